# Optimizing a Trainium2 kernel written in Bass

```python
import math
import jax, jax.numpy as jnp
from jax import lax
import numpy as np

D_MODEL = 1024
BATCH = 4
SEQ = 4096
DEPTH = 2

BLOCK = 128
EPS = 1e-6
HEAD_DIM = 64
SB_HEADS = 8
SW_HEADS = 8
SW_KV_HEADS = 2
SW_GROUP = SW_HEADS // SW_KV_HEADS
WINDOW = 128
SB_WIDTH = SB_HEADS * HEAD_DIM
SW_Q_WIDTH = SW_HEADS * HEAD_DIM
SW_KV_WIDTH = SW_KV_HEADS * HEAD_DIM
EVEN_SPLITS = [SB_WIDTH, SB_WIDTH, SB_WIDTH, SW_Q_WIDTH, SW_KV_WIDTH, SW_KV_WIDTH]
EVEN_IN_WIDTH = sum(EVEN_SPLITS)
EVEN_OUT_WIDTH = SB_WIDTH + SW_Q_WIDTH
REL_BUCKETS = 32
REL_MAX_DIST = 128
MLA_HEADS = 16
MLA_NOPE_DIM = 64
MLA_ROPE_DIM = 32
MLA_V_DIM = 64
MLA_Q_RANK = 384
MLA_KV_RANK = 256
MLA_DOWN_WIDTH = MLA_Q_RANK + MLA_KV_RANK + MLA_ROPE_DIM
ROPE_THETA = 10000.0
FFN_HIDDEN = -(-8 * D_MODEL // (3 * 256)) * 256
N_EVEN = (DEPTH + 1) // 2
N_ODD = DEPTH // 2

kernel_name = "hybrid_stickbreak_swa_mla_block"

F32 = jnp.float32


def rmsnorm(x, g):
    xf = x.astype(F32)
    y = xf * lax.rsqrt(jnp.mean(xf * xf, axis=-1, keepdims=True) + EPS)
    return (y * g.astype(F32)).astype(x.dtype)


def t5_bucket(rel):
    max_exact = REL_BUCKETS // 2
    rel = jnp.maximum(rel, 0)
    relf = jnp.maximum(rel, 1).astype(F32)
    large = max_exact + (jnp.log(relf / max_exact) / math.log(REL_MAX_DIST / max_exact)
                         * (REL_BUCKETS - max_exact)).astype(jnp.int32)
    large = jnp.minimum(large, REL_BUCKETS - 1)
    return jnp.where(rel < max_exact, rel, large)


def stick_breaking_attention(q, k, v):
    B_, S, H, d = q.shape
    nb = S // BLOCK
    scale = d ** -0.5
    qb = q.reshape(B_, nb, BLOCK, H, d).transpose(1, 0, 3, 2, 4)
    key_pos = jnp.arange(S)

    def one_block(args):
        q_blk, i = args
        z = jnp.einsum('bhqd,bshd->bhqs', q_blk, k, preferred_element_type=F32) * scale
        q_pos = i * BLOCK + jnp.arange(BLOCK)
        causal = key_pos[None, :] < q_pos[:, None]
        log_beta = jax.nn.log_sigmoid(z)
        log_1m_beta = jnp.where(causal, jax.nn.log_sigmoid(-z), 0.0)
        between = lax.cumsum(log_1m_beta, axis=3, reverse=True) - log_1m_beta
        w = jnp.where(causal, jnp.exp(log_beta + between), 0.0)
        return jnp.einsum('bhqs,bshd->bqhd', w.astype(v.dtype), v)

    out = lax.map(one_block, (qb, jnp.arange(nb)))
    return out.transpose(1, 0, 2, 3, 4).reshape(B_, S, H * d)


def sliding_window_attention(q, k, v, sinks, rel_bias_table):
    B_, S, Hq, d = q.shape
    Hkv = k.shape[2]
    G = Hq // Hkv
    nb = S // BLOCK
    qb = q.reshape(B_, nb, BLOCK, Hkv, G, d)

    def banded(t):
        tb = t.reshape(B_, nb, BLOCK, Hkv, d)
        prev = jnp.concatenate([jnp.zeros_like(tb[:, :1]), tb[:, :-1]], axis=1)
        return jnp.concatenate([prev, tb], axis=2)

    kb, vb = banded(k), banded(v)
    logits = jnp.einsum('bnqkgd,bnskd->bnkgqs', qb, kb, preferred_element_type=F32) * d ** -0.5
    rel = BLOCK + jnp.arange(BLOCK)[:, None] - jnp.arange(2 * BLOCK)[None, :]
    in_window = (rel >= 0) & (rel < WINDOW)
    key_pos = (jnp.arange(nb)[:, None] - 1) * BLOCK + jnp.arange(2 * BLOCK)[None, :]
    valid = in_window[None] & (key_pos >= 0)[:, None, :]
    bias = rel_bias_table.astype(F32)[t5_bucket(rel)]
    bias = bias.transpose(2, 0, 1).reshape(Hkv, G, BLOCK, 2 * BLOCK)
    logits = jnp.where(valid[None, :, None, None], logits + bias, -jnp.inf)
    sink = sinks.astype(F32).reshape(Hkv, G)[:, :, None, None]
    m = jnp.maximum(jnp.max(logits, axis=-1, keepdims=True), sink)
    p = jnp.exp(logits - m)
    p = p / (jnp.sum(p, axis=-1, keepdims=True) + jnp.exp(sink - m))
    out = jnp.einsum('bnkgqs,bnskd->bnqkgd', p.astype(v.dtype), vb)
    return out.reshape(B_, S, Hq * d)


def rope(x, positions):
    half = x.shape[-1] // 2
    freqs = ROPE_THETA ** (-jnp.arange(half, dtype=F32) / half)
    ang = positions.astype(F32)[..., None] * freqs
    cos = jnp.cos(ang)[:, :, None, :]
    sin = jnp.sin(ang)[:, :, None, :]
    x1 = x[..., :half].astype(F32)
    x2 = x[..., half:].astype(F32)
    return jnp.concatenate([x1 * cos - x2 * sin, x2 * cos + x1 * sin], axis=-1).astype(x.dtype)


def mla_attention(q_nope, q_rope, k_nope, k_rope, v):
    B_, S, H, _ = q_nope.shape
    nb = S // BLOCK
    scale = (MLA_NOPE_DIM + MLA_ROPE_DIM) ** -0.5
    qn = q_nope.reshape(B_, nb, BLOCK, H, MLA_NOPE_DIM).transpose(1, 0, 2, 3, 4)
    qr = q_rope.reshape(B_, nb, BLOCK, H, MLA_ROPE_DIM).transpose(1, 0, 2, 3, 4)
    key_pos = jnp.arange(S)

    def one_block(args):
        qn_blk, qr_blk, i = args
        s = (jnp.einsum('bqhd,bshd->bhqs', qn_blk, k_nope, preferred_element_type=F32)
             + jnp.einsum('bqhr,bsr->bhqs', qr_blk, k_rope, preferred_element_type=F32)) * scale
        q_pos = i * BLOCK + jnp.arange(BLOCK)
        s = jnp.where(key_pos[None, :] <= q_pos[:, None], s, -jnp.inf)
        p = jax.nn.softmax(s, axis=-1)
        return jnp.einsum('bhqs,bshd->bqhd', p.astype(v.dtype), v)

    out = lax.map(one_block, (qn, qr, jnp.arange(nb)))
    return out.transpose(1, 0, 2, 3, 4).reshape(B_, S, H * MLA_V_DIM)


def even_mixer(h, w_in, sinks, rel_bias_table, w_out):
    B_, S, _ = h.shape
    proj = h @ w_in
    q_a, k_a, v_a, q_b, k_b, v_b = jnp.split(proj, list(np.cumsum(EVEN_SPLITS)[:-1]), axis=-1)
    hd = lambda t, n: t.reshape(B_, S, n, HEAD_DIM)
    o_a = stick_breaking_attention(hd(q_a, SB_HEADS), hd(k_a, SB_HEADS), hd(v_a, SB_HEADS))
    o_b = sliding_window_attention(hd(q_b, SW_HEADS), hd(k_b, SW_KV_HEADS), hd(v_b, SW_KV_HEADS),
                                   sinks, rel_bias_table)
    return jnp.concatenate([o_a, o_b], axis=-1) @ w_out


def mla_mixer(h, positions, w_down, q_norm, w_uq, kv_norm, w_ukv, w_o):
    B_, S, _ = h.shape
    down = h @ w_down
    c_q = down[..., :MLA_Q_RANK]
    c_kv = down[..., MLA_Q_RANK:MLA_Q_RANK + MLA_KV_RANK]
    k_rope = down[..., MLA_Q_RANK + MLA_KV_RANK:]
    q = (rmsnorm(c_q, q_norm) @ w_uq).reshape(B_, S, MLA_HEADS, MLA_NOPE_DIM + MLA_ROPE_DIM)
    q_nope = q[..., :MLA_NOPE_DIM]
    q_rope = rope(q[..., MLA_NOPE_DIM:], positions)
    kv = (rmsnorm(c_kv, kv_norm) @ w_ukv).reshape(B_, S, MLA_HEADS, MLA_NOPE_DIM + MLA_V_DIM)
    k_nope = kv[..., :MLA_NOPE_DIM]
    v = kv[..., MLA_NOPE_DIM:]
    k_rope = rope(k_rope[:, :, None, :], positions)[:, :, 0, :]
    return mla_attention(q_nope, q_rope, k_nope, k_rope, v) @ w_o


def swiglu(h, w_gate, w_up, w_down):
    return (jax.nn.silu(h @ w_gate) * (h @ w_up)) @ w_down


def setup_inputs(seed: int = 0) -> dict:
    key = jax.random.key(seed)
    ks = jax.random.split(key, 20)
    nrm = lambda k, shape, fan_in: jax.random.normal(k, shape, F32) * fan_in ** -0.5
    gain = lambda k, shape: 1.0 + 0.02 * jax.random.normal(k, shape, F32)
    x = jax.random.normal(ks[0], (BATCH, SEQ, D_MODEL), F32)
    offsets = jax.random.randint(ks[1], (BATCH, 1), 0, 1024, dtype=jnp.int32)
    positions = (jnp.arange(SEQ, dtype=jnp.int32)[None, :] + offsets).astype(jnp.int32)
    return {
        "x": x,
        "positions": positions,
        "attn_norm": gain(ks[2], (DEPTH, D_MODEL)),
        "ffn_norm": gain(ks[3], (DEPTH, D_MODEL)),
        "even_w_in": nrm(ks[4], (N_EVEN, D_MODEL, EVEN_IN_WIDTH), D_MODEL),
        "even_sinks": 0.5 * jax.random.normal(ks[5], (N_EVEN, SW_HEADS), F32),
        "even_w_out": nrm(ks[6], (N_EVEN, EVEN_OUT_WIDTH, D_MODEL), EVEN_OUT_WIDTH),
        "rel_bias_table": 0.5 * jax.random.normal(ks[7], (REL_BUCKETS, SW_HEADS), F32),
        "mla_w_down": nrm(ks[8], (N_ODD, D_MODEL, MLA_DOWN_WIDTH), D_MODEL),
        "mla_q_norm": gain(ks[9], (N_ODD, MLA_Q_RANK)),
        "mla_w_uq": nrm(ks[10], (N_ODD, MLA_Q_RANK, MLA_HEADS * (MLA_NOPE_DIM + MLA_ROPE_DIM)), MLA_Q_RANK),
        "mla_kv_norm": gain(ks[11], (N_ODD, MLA_KV_RANK)),
        "mla_w_ukv": nrm(ks[12], (N_ODD, MLA_KV_RANK, MLA_HEADS * (MLA_NOPE_DIM + MLA_V_DIM)), MLA_KV_RANK),
        "mla_w_o": nrm(ks[13], (N_ODD, MLA_HEADS * MLA_V_DIM, D_MODEL), MLA_HEADS * MLA_V_DIM),
        "ffn_w_gate": nrm(ks[14], (DEPTH, D_MODEL, FFN_HIDDEN), D_MODEL),
        "ffn_w_up": nrm(ks[15], (DEPTH, D_MODEL, FFN_HIDDEN), D_MODEL),
        "ffn_w_down": nrm(ks[16], (DEPTH, FFN_HIDDEN, D_MODEL), FFN_HIDDEN),
        "final_norm": gain(ks[17], (D_MODEL,)),
    }


def reference(x, positions, attn_norm, ffn_norm, even_w_in, even_sinks, even_w_out, rel_bias_table,
              mla_w_down, mla_q_norm, mla_w_uq, mla_kv_norm, mla_w_ukv, mla_w_o,
              ffn_w_gate, ffn_w_up, ffn_w_down, final_norm):
    for layer in range(DEPTH):
        h = rmsnorm(x, attn_norm[layer])
        if layer % 2 == 0:
            e = layer // 2
            x = x + even_mixer(h, even_w_in[e], even_sinks[e], rel_bias_table, even_w_out[e])
        else:
            o = layer // 2
            x = x + mla_mixer(h, positions, mla_w_down[o], mla_q_norm[o], mla_w_uq[o],
                              mla_kv_norm[o], mla_w_ukv[o], mla_w_o[o])
        h = rmsnorm(x, ffn_norm[layer])
        x = x + swiglu(h, ffn_w_gate[layer], ffn_w_up[layer], ffn_w_down[layer])
    return rmsnorm(x, final_norm)
```

```python
from contextlib import ExitStack
import math
import numpy as np
import concourse.bass as bass
import concourse.mybir as mybir
from concourse.bass_utils import run_bass_kernel_spmd

F32 = mybir.dt.float32
BF16 = mybir.dt.bfloat16
I32 = mybir.dt.int32
AF = mybir.ActivationFunctionType
ALU = mybir.AluOpType

NEG = -30000.0
NB = 16
NBA = 32
FF = 2816
NFB = FF // 128
ARENA_F32 = 52800
LATW = 448
import os
STOP = int(os.environ.get('KSTOP', '99'))
STOP1 = int(os.environ.get('KSTOP1', '99'))
SUB = int(os.environ.get('KSUB', '99'))
PI = math.pi


class Buf:
    __slots__ = ("w", "r", "excl")

    def __init__(self, excl=False):
        self.w = None
        self.r = []
        self.excl = excl


class Sched:
    ENG = ("pe", "act", "dve", "pool", "sp")

    def __init__(self, nc, stack):
        self.nc = nc
        self.stack = stack
        self.sem = {e: stack.enter_context(nc.semaphore("sem_" + e)) for e in self.ENG}
        self.cnt = {e: 0 for e in self.ENG}
        self.ops = {e: [] for e in self.ENG}
        self.waited = {e: {} for e in self.ENG}
        self.dsems = []
        self.lazy = set()

    def dma_sem(self):
        s = self.stack.enter_context(self.nc.semaphore("dsem%d" % len(self.dsems)))
        d = [s, 0]
        self.dsems.append(d)
        return d

    def _waits(self, eng, deps):
        out = []
        for ev in deps:
            if ev is None:
                continue
            key, sem, val, src = ev
            if src == "pe" and eng == "pe":
                continue
            if self.waited[eng].get(key, 0) >= val:
                continue
            self.waited[eng][key] = val
            out.append((sem, val))
        return out

    def _deps(self, reads, writes, extra, eng=None):
        deps = list(extra)
        for b in reads:
            deps.append(b.w)
            if b.excl:
                deps.extend(r for r in b.r if r[3] != eng)
        for b in writes:
            deps.append(b.w)
            deps.extend(b.r)
        return deps

    def _mark(self, ev, reads, writes):
        for b in reads:
            b.r.append(ev)
        for b in writes:
            b.w = ev
            b.r = []

    def op(self, eng, fn, reads=(), writes=(), extra=()):
        waits = self._waits(eng, self._deps(reads, writes, extra, eng))
        self.cnt[eng] += 1
        ev = (eng, self.sem[eng], self.cnt[eng], eng)
        self.ops[eng].append((waits, fn, (self.sem[eng], 1)))
        self._mark(ev, reads, writes)
        return ev

    def dma(self, q, dsem, out, in_, reads=(), writes=(), extra=()):
        waits = self._waits(q, self._deps(reads, writes, extra, q))
        dsem[1] += 16
        ev = (id(dsem), dsem[0], dsem[1], "dma")
        self.ops[q].append((waits, (lambda e, o=out, i=in_: e.dma_start(out=o, in_=i)), (dsem[0], 16)))
        self._mark(ev, reads, writes)
        return ev

    def wait_on(self, eng, evs):
        waits = self._waits(eng, evs)
        if waits:
            self.ops[eng].append((waits, None, None))

    def barrier(self):
        evs = [(e, self.sem[e], self.cnt[e], e) for e in self.ENG if self.cnt[e] > 0]
        evs += [(id(d), d[0], d[1], "dma") for d in self.dsems if d[1] > 0 and id(d) not in self.lazy]
        for e in self.ENG:
            self.wait_on(e, [ev for ev in evs if not (ev[3] == "pe" and e == "pe")])

    def flush(self, block):
        def mk(e):
            def run(engine):
                for waits, fn, inc in self.ops[e]:
                    for s, v in waits:
                        engine.wait_ge(s, v)
                    if fn is not None:
                        fn(engine).then_inc(inc[0], inc[1])
            return run
        block.tensor(mk("pe"))
        block.scalar(mk("act"))
        block.vector(mk("dve"))
        block.gpsimd(mk("pool"))
        block.sync(mk("sp"))


def fgroups():
    out = []
    fb = 0
    while fb < NFB:
        n = min(4, NFB - fb)
        out.append((fb, n))
        fb += n
    return out


def build(mode):
    nc = bass.Bass("TRN2", target_bir_lowering=False)
    do0 = mode in ("A", "F", "R")
    do1 = mode in ("B", "F", "R")

    def din(name, shape, dt=F32):
        return nc.dram_tensor(name, list(shape), dt, kind="ExternalInput").ap()

    def dout(name, shape, dt=F32):
        return nc.dram_tensor(name, list(shape), dt, kind="ExternalOutput").ap()

    def dint(name, shape, dt=F32):
        return nc.dram_tensor(name, list(shape), dt, kind="Internal").ap()

    gvec = din("gvec", [5, 1024])
    ident_d = din("ident", [128, 128])
    ffn_g = din("ffn_g", [2, 1024, FF])
    ffn_u = din("ffn_u", [2, 1024, FF])
    ffn_d = din("ffn_d", [2, FF, 1024])
    if do0:
        x_own = din("x_own", [NB, 128, 1024])
        x_all = din("x_all", [NBA, 128, 1024])
        w_kv0 = din("w_kv0", [1024, 1664])
        w_q0 = din("w_q0", [1024, 1024])
        w_out0 = din("w_out0", [1024, 1024])
        tri_d = din("negtri", [128, 128])
        sbmask_d = din("sbmask", [128, 256])
        swa_bias = din("swa_bias", [128, 6 * 512])
        swa_mask = din("swa_mask", [128, 6 * 512])
        sinks_d = din("sinks", [1, 8])
    if do1:
        mla_dn = din("mla_dn", [1024, 832])
        mla_uq = din("mla_uq", [384, 1568])
        mla_uqs = din("mla_uqs", [384, 1568])
        mla_kn = din("mla_kn", [256, 1024])
        mla_v = din("mla_v", [256, 1024])
        mla_wo = din("mla_wo", [1024, 1024])
        gq_d = din("gq", [1, 384])
        gkv_d = din("gkv", [1, 256])
        mlamask_d = din("mlamask", [128, 256])
        rope_cst_d = din("rope_cst", [128, 8])
        pos_own = din("pos_own", [1, NB * 128], I32)
        pos_all = din("pos_all", [1, NBA * 128], I32)
        out_own = dout("out_own", [NB, 128, 1024])
    if mode == "R":
        x_oth = din("x_oth", [NB, 128, 1024])
        sbmask2_d = din("sbmask2", [128, 256])
        swa_bias2 = din("swa_bias2", [128, 6 * 512])
        swa_mask2 = din("swa_mask2", [128, 6 * 512])
        x1_d = dint("x1_own", [NB, 128, 1024])
        x1_o = dint("x1_oth", [NB, 128, 1024])
        lat_all = dint("lat_all", [NBA, 128, LATW])
    elif mode == "A":
        x1_d = dout("x1_own", [NB, 128, 1024])
    elif mode == "B":
        x1_d = din("x1_own", [NB, 128, 1024])
        x1_eo = din("x1_eo", [NBA, 128, 1024])
        lat_all = dint("lat_all", [NBA, 128, LATW])
    else:
        x1_d = dint("x1_own", [NB, 128, 1024])
        lat_own = dint("lat_own", [NB * 128, LATW])
        if os.environ.get("KCC") == "8":
            CCG = [[0, 1, 2, 3, 4, 5, 6, 7]]
            lat_cc = dint("lat_cc", [8 * NB * 128, LATW])
            lat_all2 = lat_cc[0:NBA * 128, :]
        elif os.environ.get("KCC") == "4":
            CCG = [[0, 1, 2, 3], [4, 5, 6, 7]]
            lat_cc = dint("lat_cc", [4 * NB * 128, LATW])
            lat_all2 = lat_cc[0:NBA * 128, :]
        else:
            CCG = [[0, 1], [2, 3], [4, 5], [6, 7]]
            lat_cc = dint("lat_all", [NBA * 128, LATW])
            lat_all2 = lat_cc
        lat_all = lat_all2.rearrange("(b p) n -> b p n", p=128)

    with ExitStack() as top:
        S = Sched(nc, top)

        arena = top.enter_context(nc.sbuf_tensor("arena", [128, ARENA_F32], F32))

        class Arena:
            def __init__(self):
                self.off = 0
                self.peak = 0

            def mark(self):
                return self.off

            def release(self, m):
                self.off = m

            def alloc(self, shape, dt):
                esz = 4 if dt in (F32, I32) else 2
                n = 1
                for d_ in shape[1:]:
                    n *= d_
                nbytes = (n * esz + 63) // 64 * 64
                o = self.off
                self.off += nbytes
                self.peak = max(self.peak, self.off)
                assert self.off <= ARENA_F32 * 4, ("SBUF arena overflow", self.off)
                v = arena[:, o // 4:(o + nbytes) // 4]
                if dt != F32:
                    v = v.bitcast(dt)
                v = v[0:shape[0], 0:n]
                if len(shape) == 3:
                    v = v.rearrange("p (a b) -> p a b", a=shape[1])
                elif len(shape) == 4:
                    v = v.rearrange("p (a b c) -> p a b c", a=shape[1], b=shape[2])
                return v

        AR = Arena()

        class _Region:
            def __init__(self, lo, hi):
                self.lo, self.hi = lo, hi

            def __enter__(self):
                self.saved = AR.off
                AR.off = self.lo
                return self

            def __exit__(self, *a):
                assert AR.off <= self.hi, ("region overflow", AR.off, self.hi)
                AR.off = self.saved
                return False

        class _Scope:
            def __enter__(self):
                self.m = AR.mark()
                return self

            def __exit__(self, *a):
                AR.release(self.m)
                return False

        def sb(st, name, shape, dt):
            return AR.alloc(list(shape), dt)

        ps = [top.enter_context(nc.psum_tensor("ps%d" % i, [128, 512], F32)) for i in range(8)]
        pb = [Buf(excl=True) for _ in range(8)]
        block = top.enter_context(nc.Block())

        ident = sb(None, "ident", [128, 128], BF16)
        onesf = sb(None, "onesf", [128, 64], F32)
        cd = S.dma_sem()
        cdp = S.dma_sem()
        S.dma("pool", cdp, ident[:], ident_d[:, :])
        S.op("pool", lambda e: e.memset(onesf[:], 1.0))
        S.barrier()

        class NormCtx:
            def __init__(self, st, tag, n, nslots=2):
                self.n = n
                self.junk = sb(st, tag + "_junk", [128, n], BF16)
                self.bjunk = Buf()
                self.ss = [sb(st, "%s_ss%d" % (tag, i), [128, 2], F32) for i in range(nslots)]
                self.bss = [Buf() for _ in range(nslots)]
                self.i = 0

        def rmsnorm(nx, src_ap, src_bufs, g_ap, out_ap, out_bufs, dim, eng_scale="dve"):
            k = nx.i % len(nx.ss)
            nx.i += 1
            ss, bss = nx.ss[k], nx.bss[k]
            S.op("act", lambda e: e.activation(out=nx.junk[:, 0:dim], in_=src_ap, func=AF.Square, accum_out=ss[:, 0:1]),
                 reads=src_bufs, writes=[nx.bjunk, bss])
            S.op("act", lambda e: e.activation(out=ss[:, 1:2], in_=ss[:, 0:1], func=AF.Ln, scale=1.0 / dim, bias=1e-6),
                 reads=[], writes=[bss])
            S.op("act", lambda e: e.activation(out=ss[:, 1:2], in_=ss[:, 1:2], func=AF.Exp, scale=-0.5),
                 reads=[], writes=[bss])
            return S.op(eng_scale, lambda e: e.scalar_tensor_tensor(out=out_ap, in0=src_ap, scalar=ss[:, 1:2], in1=g_ap,
                                                                    op0=ALU.mult, op1=ALU.mult),
                        reads=list(src_bufs) + [bss], writes=out_bufs)

        evac_toggle = [0]

        def evac(out_ap, in_ap, reads, writes, scale=None, eng=None):
            if eng is None:
                eng = ("dve", "act")[evac_toggle[0] % 2]
                evac_toggle[0] += 1
            if eng == "act":
                if scale is None:
                    return S.op("act", lambda e: e.activation(out=out_ap, in_=in_ap, func=AF.Copy), reads=reads, writes=writes)
                return S.op("act", lambda e: e.activation(out=out_ap, in_=in_ap, func=AF.Copy, scale=float(scale)), reads=reads, writes=writes)
            if scale is None:
                return S.op(eng, lambda e: e.tensor_copy(out=out_ap, in_=in_ap), reads=reads, writes=writes)
            return S.op(eng, lambda e: e.tensor_scalar(out=out_ap, in0=in_ap, scalar1=float(scale), scalar2=None, op0=ALU.mult),
                        reads=reads, writes=writes)

        def load_gbc(st, name, row):
            t = sb(st, name, [128, 1024], F32)
            S.dma("sp", cd, t[:], gvec[row:row + 1, :].partition_broadcast(128))
            return t

        def mm(out, lhsT, rhs, start, stop, reads, writes, **kw):
            return S.op("pe", lambda e: e.matmul(out, lhsT, rhs, start=start, stop=stop, **kw), reads=reads, writes=writes)

        class TokPass:
            def __init__(self, st, tag, gbc, bank0, hole=None):
                self.gbc = gbc
                if hole is not None:
                    with _Region(hole[0], hole[1]):
                        self.xst = [sb(st, "%s_x%d" % (tag, i), [128, 1024], F32) for i in range(3)]
                        self.hT = [sb(st, "%s_hT%d" % (tag, i), [128, 8, 512], BF16) for i in range(2)]
                        self.hb = [sb(st, "%s_hb%d" % (tag, i), [128, 1024], BF16) for i in range(2)]
                else:
                    self.hb = [sb(st, "%s_hb%d" % (tag, i), [128, 1024], BF16) for i in range(2)]
                    self.xst = [sb(st, "%s_x%d" % (tag, i), [128, 1024], F32) for i in range(3)]
                    self.hT = [sb(st, "%s_hT%d" % (tag, i), [128, 8, 512], BF16) for i in range(2)]
                self.bx = [Buf() for _ in range(3)]
                self.dx = [S.dma_sem() for _ in range(3)]
                self.bhb = [Buf() for _ in range(2)]
                self.bhT = [[Buf() for _ in range(4)] for _ in range(2)]
                self.nx = NormCtx(st, tag + "_n", 1024, 3)
                self.bank0 = bank0
                self.nblk = 0
                self.nchunk = 0

            def stage1(self, x_sb_ap, x_bufs):
                i = self.nblk
                self.nblk += 1
                hb, bhb = self.hb[i % 2], self.bhb[i % 2]
                rmsnorm(self.nx, x_sb_ap, x_bufs, self.gbc[:], hb[:], [bhb], 1024)
                return i

            def stage2(self, i, j, cslot):
                hb, bhb = self.hb[i % 2], self.bhb[i % 2]
                bk = self.bank0 + (i % 2)
                psT = ps[bk][:].bitcast(BF16)
                for k in range(8):
                    S.op("pe", lambda e, k=k: e.transpose(psT[:, k * 128:(k + 1) * 128], hb[:, k * 128:(k + 1) * 128], ident[:]),
                         reads=[bhb], writes=[pb[bk]])
                evac(self.hT[cslot][:, :, j * 128:(j + 1) * 128], psT.rearrange("p (k t) -> p k t", k=8),
                     [pb[bk]], [self.bhT[cslot][j]])

            def chunk(self, src_blocks_dram, c, nblk=4):
                cslot = self.nchunk % 2
                self.nchunk += 1
                pend = []
                for j in range(nblk):
                    sl = self.nblk % 3
                    S.dma("sp", self.dx[sl], self.xst[sl][:], src_blocks_dram[4 * c + j], writes=[self.bx[sl]])
                    i = self.stage1(self.xst[sl][:], [self.bx[sl]])
                    pend.append((i, j))
                    if len(pend) > 1:
                        self.stage2(*pend.pop(0), cslot)
                for p_ in pend:
                    self.stage2(*p_, cslot)
                return self.hT[cslot], self.bhT[cslot]

        def load_w(st, name, dram2d, kchunks, ncols, c0=0, dsem=None):
            t = sb(st, name, [128, kchunks, ncols], BF16)
            v = dram2d.rearrange("(k p) n -> p k n", p=128)
            b = Buf()
            ev = None
            for k0 in range(0, kchunks, 4):
                k1 = min(kchunks, k0 + 4)
                ev = S.dma("pool", dsem or cdp, t[:, k0:k1, :], v[:, k0:k1, c0:c0 + ncols])
            b.w = ev
            return t, b

        def ffn_layer(st, l, xres, bxres, hole):
            with _Region(hole[0], hole[1]):
                hT = sb(st, "ffn_hT", [128, 8, 512], BF16)
                aT = sb(st, "ffn_aT", [128, NFB, 512], BF16)
            gbc = load_gbc(st, "ffn_gbc", 1 + 2 * l)
            wd = sb(st, "ffn_wd", [128, NFB, 1024], BF16)
            bwd = Buf()
            dwd = S.dma_sem()
            wdview = ffn_d[l].rearrange("(k p) n -> p k n", p=128)
            gview = ffn_g[l].rearrange("(k p) n -> p k n", p=128)
            uview = ffn_u[l].rearrange("(k p) n -> p k n", p=128)
            wg = [sb(st, "ffn_wg%d" % i, [128, 8, 512], BF16) for i in range(2)]
            wu = [sb(st, "ffn_wu%d" % i, [128, 8, 512], BF16) for i in range(2)]
            bwg = [Buf() for _ in range(2)]
            bwu = [Buf() for _ in range(2)]
            dwg = [S.dma_sem() for _ in range(2)]
            dwu = [S.dma_sem() for _ in range(2)]
            hb = [sb(st, "ffn_hb%d" % i, [128, 1024], BF16) for i in range(2)]
            bhb = [Buf() for _ in range(2)]
            bhT = [Buf() for _ in range(4)]
            baT = [Buf() for _ in range(NFB)]
            sg = [sb(st, "ffn_sg%d" % i, [128, 512], F32) for i in range(2)]
            bsg = [Buf() for _ in range(2)]
            nx = NormCtx(st, "ffn_n", 1024, 3)
            S.barrier()
            nstream = 0
            nfb = 0
            nblk = 0
            ndown = 0
            for c in range(4):
                def ffn_stage2(j, k2):
                    bk = k2
                    psT = ps[bk][:].bitcast(BF16)
                    for k in range(8):
                        S.op("pe", lambda e, k=k, psT=psT, h=hb[k2]: e.transpose(psT[:, k * 128:(k + 1) * 128], h[:, k * 128:(k + 1) * 128], ident[:]),
                             reads=[bhb[k2]], writes=[pb[bk]])
                    evac(hT[:, :, j * 128:(j + 1) * 128], psT.rearrange("p (k t) -> p k t", k=8), [pb[bk]], [bhT[j]])
                pend = []
                for j in range(4):
                    blk = 4 * c + j
                    k2 = nblk % 2
                    nblk += 1
                    rmsnorm(nx, xres[:, blk, :], [bxres[blk]], gbc[:], hb[k2][:], [bhb[k2]], 1024)
                    pend.append((j, k2))
                    if len(pend) > 1:
                        ffn_stage2(*pend.pop(0))
                for p_ in pend:
                    ffn_stage2(*p_)
                for (fb0, nf) in fgroups():
                    sl = nstream % 2
                    nstream += 1
                    S.dma("pool", dwg[sl], wg[sl][:, :, 0:nf * 128], gview[:, :, fb0 * 128:(fb0 + nf) * 128], writes=[bwg[sl]])
                    S.dma("pool", dwu[sl], wu[sl][:, :, 0:nf * 128], uview[:, :, fb0 * 128:(fb0 + nf) * 128], writes=[bwu[sl]])
                    if c == 0 and nstream == 2:
                        ev_ = None
                        for k0 in range(0, NFB, 4):
                            k1 = min(NFB, k0 + 4)
                            ev_ = S.dma("pool", dwd, wd[:, k0:k1, :], wdview[:, k0:k1, :])
                        bwd.w = ev_
                    for f in range(nf):
                        fb = fb0 + f
                        gbk = 2 + (nfb % 2)
                        ubk = 4 + (nfb % 2)
                        s2 = nfb % 2
                        nfb += 1
                        for k in range(8):
                            mm(ps[gbk][:], wg[sl][:, k, f * 128:(f + 1) * 128], hT[:, k, :], k == 0, k == 7, [bwg[sl]] + bhT, [pb[gbk]])
                        for k in range(8):
                            mm(ps[ubk][:], wu[sl][:, k, f * 128:(f + 1) * 128], hT[:, k, :], k == 0, k == 7, [bwu[sl]] + bhT, [pb[ubk]])
                        S.op("act", lambda e, o=sg[s2], i=ps[gbk]: e.activation(out=o[:], in_=i[:], func=AF.Silu),
                             reads=[pb[gbk]], writes=[bsg[s2]])
                        S.op("dve", lambda e, o=aT[:, fb, :], a=sg[s2], b=ps[ubk]: e.tensor_tensor(out=o, in0=a[:], in1=b[:], op=ALU.mult),
                             reads=[bsg[s2], pb[ubk]], writes=[baT[fb]])
                for j in range(4):
                    blk = 4 * c + j
                    for half in range(2):
                        bk = 6 + (ndown % 2)
                        ndown += 1
                        for fb in range(NFB):
                            mm(ps[bk][:], aT[:, fb, j * 128:(j + 1) * 128], wd[:, fb, half * 512:(half + 1) * 512],
                               fb == 0, fb == NFB - 1, [baT[fb], bwd], [pb[bk]])
                        S.op("dve", lambda e, o=xres[:, blk, half * 512:(half + 1) * 512], p_=ps[bk]: e.tensor_tensor(out=o, in0=o, in1=p_[:], op=ALU.add),
                             reads=[pb[bk]], writes=[bxres[blk]])
            S.barrier()

        def outproj_residual(st, oT, w_dram, xres, bxres, tag):
            w, bw = load_w(st, tag + "_w", w_dram, 8, 1024)
            n = 0
            for blk in range(NB):
                for half in range(2):
                    bk = n % 4
                    n += 1
                    for k in range(8):
                        mm(ps[bk][:], oT[:, k, blk * 128:(blk + 1) * 128], w[:, k, half * 512:(half + 1) * 512],
                           k == 0, k == 7, [bw], [pb[bk]])
                    S.op("dve", lambda e, o=xres[:, blk, half * 512:(half + 1) * 512], p_=ps[bk]: e.tensor_tensor(out=o, in0=o, in1=p_[:], op=ALU.add),
                         reads=[pb[bk], bxres[blk]], writes=[bxres[blk]])

        def load_xres(st, src):
            xres = sb(st, "xres", [128, NB, 1024], F32)
            bx = [Buf() for _ in range(NB)]
            dl = S.dma_sem()
            ev = None
            for blk in range(NB):
                ev = S.dma("sp", dl, xres[:, blk, :], src[blk])
            for b_ in bx:
                b_.w = ev
            return xres, bx

        def layer0(x_own, sbmask_d, swa_bias, swa_mask, x1_d, kvc=None):
            with _Scope() as l0:
              hole0 = AR.mark()
              oT = sb(l0, "oT", [128, 8, NB * 128], BF16)
              hole1 = AR.mark()
              with _Scope() as att:
                kTa = sb(l0, "kTa", [128, 4, NBA * 128], BF16)
                va = sb(l0, "va", [128, NBA, 512], BF16)
                qTa = sb(l0, "qTa", [128, 4, NB * 128], BF16)
                att_sb_mark = AR.mark()
                kz = sb(l0, "kz", [128, 4, NBA * 128], BF16)
                vb = sb(l0, "vb", [128, NBA, 2, 72], BF16)
                qTb = sb(l0, "qTb", [128, 4, NB * 128], BF16)
                S.op("pool", lambda e: e.memset(vb[:], 1.0))
                S.barrier()
                if kvc is not None and kvc[0] == "load":
                    dkv = S.dma_sem()
                    S.lazy.add(id(dkv))
                    for t_sb, t_dr in zip((kTa, va, kz, vb), kvc[1]):
                        for k0 in range(0, t_sb.shape[1], 8):
                            k1 = min(t_sb.shape[1], k0 + 8)
                            S.dma("sp", dkv, t_sb[:, k0:k1], t_dr[:, k0:k1])
                with _Scope() as st:
                  if not (kvc is not None and kvc[0] == "load"):
                    gbc = load_gbc(st, "gbc_a0", 0)
                    wkv, bwkv = load_w(st, "wkv", w_kv0, 8, 1664)
                    tp = TokPass(st, "p1a", gbc, 0, (hole0, hole1))
                    S.barrier()
                    n = 0
                    for c in range(8):
                        hT, bhT = tp.chunk(x_all, c)
                        for pr in range(4):
                            bk = 2 + n % 4
                            n += 1
                            for k in range(8):
                                mm(ps[bk][:], wkv[:, k, pr * 128:(pr + 1) * 128], hT[:, k, :], k == 0, k == 7, bhT, [pb[bk]])
                            evac(kTa[:, pr, c * 512:(c + 1) * 512], ps[bk][:], [pb[bk]], [])
                        for g in range(4):
                            bk = 2 + n % 4
                            n += 1
                            for k in range(8):
                                mm(ps[bk][:], wkv[:, k, 1024 + g * 128:1024 + (g + 1) * 128], hT[:, k, :], k == 0, k == 7, bhT, [pb[bk]])
                            evac(kz[:, g, c * 512:(c + 1) * 512], ps[bk][:], [pb[bk]], [])
                        for j in range(4):
                            gbk = 4 * c + j
                            bk = 2 + n % 4
                            n += 1
                            for k in range(8):
                                mm(ps[bk][:], hT[:, k, j * 128:(j + 1) * 128], wkv[:, k, 512:1024], k == 0, k == 7, bhT, [pb[bk]])
                            evac(va[:, gbk, :], ps[bk][:], [pb[bk]], [])
                            bk = 2 + n % 4
                            n += 1
                            for k in range(8):
                                mm(ps[bk][:, 0:128], hT[:, k, j * 128:(j + 1) * 128], wkv[:, k, 1536:1664], k == 0, k == 7, bhT, [pb[bk]])
                            evac(vb[:, gbk, :, 0:64], ps[bk][:, 0:128].rearrange("p (g d) -> p g d", g=2), [pb[bk]], [])
                    S.barrier()
                    if kvc is not None and kvc[0] == "save":
                        dkv = S.dma_sem()
                        S.lazy.add(id(dkv))
                        for t_sb, t_dr in zip((kTa, va, kz, vb), kvc[1]):
                            for k0 in range(0, t_sb.shape[1], 8):
                                k1 = min(t_sb.shape[1], k0 + 8)
                                S.dma("sp", dkv, t_dr[:, k0:k1], t_sb[:, k0:k1])
                with _Scope() as st:
                  if STOP >= 2:
                    gbc = load_gbc(st, "gbc_a0b", 0)
                    wq, bwq = load_w(st, "wq", w_q0, 8, 1024)
                    tp = TokPass(st, "p1b", gbc, 0, (hole0, hole1))
                    S.barrier()
                    n = 0
                    for c in range(4):
                        hT, bhT = tp.chunk(x_own, c)
                        for pr in range(8):
                            bk = 2 + n % 4
                            n += 1
                            for k in range(8):
                                mm(ps[bk][:], wq[:, k, pr * 128:(pr + 1) * 128], hT[:, k, :], k == 0, k == 7, bhT, [pb[bk]])
                            dst = qTa if pr < 4 else qTb
                            evac(dst[:, pr % 4, c * 512:(c + 1) * 512], ps[bk][:], [pb[bk]], [], scale=0.125)
                    S.lazy.clear()
                    S.barrier()
                with _Scope() as st:
                  if STOP >= 4:
                    bm = sb(st, "swa_bm", [128, 6 * 512], F32)
                    ltb = sb(st, "swa_ltb", [128, 2, 3 * 512], F32)
                    mk_ = ltb.rearrange("p a n -> p (a n)")
                    S.dma("sp", cd, bm[:], swa_bias[:, :])
                    S.dma("sp", cd, mk_[:], swa_mask[:, :])
                    esk = sb(st, "esk", [128, 8], F32)
                    S.dma("sp", cd, esk[:], sinks_d[0:1, :].partition_broadcast(128))
                    S.barrier()
                    S.op("dve", lambda e: e.tensor_tensor(out=bm[:], in0=bm[:], in1=mk_[:], op=ALU.add))
                    S.op("act", lambda e: e.activation(out=esk[:], in_=esk[:], func=AF.Exp))
                    S.barrier()
                    lt = [ltb[:, i, :] for i in range(2)]
                    blt = [Buf() for _ in range(2)]
                    ws = [sb(st, "swa_w%d" % i, [128, 3 * 512], BF16) for i in range(2)]
                    bws = [Buf() for _ in range(2)]
                    rden = sb(st, "swa_rden", [128, 512], F32)
                    brden = Buf()
                    ouf = sb(st, "swa_ouf", [128, 512], F32)
                    bouf = Buf()
                    otb = [sb(st, "swa_otb%d" % i, [128, 256], BF16) for i in range(2)]
                    botb = [Buf() for _ in range(2)]
                    dsh = [S.dma_sem() for _ in range(2)]
                    def swa_stage_a(i, g, s2):
                        rs = (1, 2) if i == 0 else (0, 1, 2)
                        sbk = (0, 1, 2) if s2 == 0 else (3, 6, 7)
                        for u in range(2):
                            for r in rs:
                                kbg = 2 * i - 1 + r
                                for a in range(2):
                                    mm(ps[sbk[r]][:, u * 256 + a * 128:u * 256 + (a + 1) * 128],
                                       kz[:, 2 * g + u, kbg * 128:(kbg + 1) * 128],
                                       qTb[:, 2 * g + a, i * 128:(i + 1) * 128],
                                       True, True, [], [pb[sbk[r]]])
                        for r in rs:
                            S.op("dve", lambda e, o=lt[s2][:, r * 512:(r + 1) * 512], p_=ps[sbk[r]], b_=bm[:, (g * 3 + r) * 512:(g * 3 + r + 1) * 512]:
                                 e.tensor_tensor(out=o, in0=p_[:], in1=b_, op=ALU.add),
                                 reads=[pb[sbk[r]]], writes=[blt[s2]])
                        lo_, hi_ = rs[0] * 512, 3 * 512
                        S.op("act", lambda e, o=ws[s2][:, lo_:hi_], i_=lt[s2][:, lo_:hi_]: e.activation(out=o, in_=i_, func=AF.Exp),
                             reads=[blt[s2]], writes=[bws[s2]])
                        ob = 4 + s2
                        for q in range(4):
                            for r in rs:
                                kbg = 2 * i - 1 + r
                                mm(ps[ob][0:65, q * 128:(q + 1) * 128], vb[:, kbg, g, 0:65], ws[s2][:, r * 512 + q * 128:r * 512 + (q + 1) * 128],
                                   r == rs[0], r == 2, [bws[s2]], [pb[ob]])

                    def swa_stage_b(i, g, s2):
                        ob = 4 + s2
                        for q in range(4):
                            hd = 4 * g + (0, 2, 1, 3)[q]
                            S.op("dve", lambda e, p_=ps[ob], q=q, hd=hd: e.tensor_scalar(out=rden[64:65, q * 128:(q + 1) * 128], in0=p_[64:65, q * 128:(q + 1) * 128],
                                                                                    scalar1=esk[64:65, hd:hd + 1], scalar2=None, op0=ALU.add),
                                 reads=[pb[ob]], writes=[brden])
                        S.op("act", lambda e: e.activation(out=rden[64:65, :], in_=rden[64:65, :], func=AF.Ln), reads=[brden], writes=[brden])
                        S.op("act", lambda e: e.activation(out=rden[64:65, :], in_=rden[64:65, :], func=AF.Exp, scale=-1.0), reads=[brden], writes=[brden])
                        S.op("act", lambda e, p_=ps[ob]: e.activation(out=ouf[0:64, :], in_=p_[0:64, :], func=AF.Copy), reads=[pb[ob]], writes=[bouf])
                        mm(ps[ob][0:64, :], onesf[64:65, 0:64], rden[64:65, :], True, True, [brden], [pb[ob]])
                        for a in range(2):
                            S.op("dve", lambda e, a=a, p_=ps[ob]: e.tensor_tensor(out=oT[0:64, 4 + 2 * g + a, i * 128:(i + 1) * 128],
                                                                            in0=ouf[0:64, a * 128:(a + 1) * 128],
                                                                            in1=p_[0:64, a * 128:(a + 1) * 128], op=ALU.mult),
                                 reads=[bouf, pb[ob]], writes=[])
                        S.op("dve", lambda e, o=otb[s2], p_=ps[ob]: e.tensor_tensor(out=o[0:64, :], in0=ouf[0:64, 256:512], in1=p_[0:64, 256:512], op=ALU.mult),
                             reads=[bouf, pb[ob]], writes=[botb[s2]])
                        S.dma("sp", dsh[s2], oT[64:128, 4 + 2 * g:4 + 2 * g + 2, i * 128:(i + 1) * 128],
                              otb[s2][0:64, :].rearrange("p (a t) -> p a t", a=2), reads=[botb[s2]])

                    its = [(i, g) for i in range(NB) for g in range(2)]
                    for n_, (i, g) in enumerate(its):
                        swa_stage_a(i, g, n_ % 2)
                        if n_ >= 1:
                            swa_stage_b(its[n_ - 1][0], its[n_ - 1][1], (n_ - 1) % 2)
                    swa_stage_b(its[-1][0], its[-1][1], (len(its) - 1) % 2)
                    S.barrier()
                AR.release(att_sb_mark)
                with _Scope() as st:
                  if STOP >= 3 and not os.environ.get('KSKIP_SB'):
                    negtri = sb(st, "negtri", [128, 128], BF16)
                    negones = sb(st, "negones", [128, 128], BF16)
                    sbmask = sb(st, "sbmask", [128, 256], BF16)
                    S.dma("pool", cdp, negtri[:], tri_d[:, :])
                    S.dma("pool", cdp, sbmask[:], sbmask_d[:, :])
                    S.op("pool", lambda e: e.memset(negones[:], -1.0))
                    sp_all = sb(st, "sp_all", [128, NBA, 512], BF16)
                    bsp = [Buf() for _ in range(NBA)]
                    lsum = sb(st, "lsum", [128, 512], BF16)
                    blsum = Buf()
                    wt = [sb(st, "wt%d" % i, [128, 512], BF16) for i in range(4)]
                    bwt = [Buf() for _ in range(4)]
                    qz = [sb(st, "qz%d" % i, [128, 512], BF16) for i in range(2)]
                    bqz = [Buf() for _ in range(2)]
                    for i_ in range(2):
                        S.op("pool", lambda e, t_=qz[i_]: e.memset(t_[:], 0.0))
                    S.barrier()
                    nA = 0
                    nB = 0
                    nW = 0
                    bankA = (0, 1, 6)
                    bankB = (2, 3, 7)
                    for c in range(4):
                        for h in range(8):
                            pr, u = h // 2, h % 2
                            r0, r1 = 64 * u, 64 * u + 64
                            nkb = 8 * c + 8
                            qs = c * 512
                            S.op("dve", lambda e, o=qz[u][r0:r1, :], i=qTa[r0:r1, pr, qs:qs + 512]: e.tensor_copy(out=o, in_=i),
                                 reads=[], writes=[bqz[u]])

                            def zmm(bank, kb, lo):
                                mm(ps[bank][:, lo:512], kTa[:, pr, kb * 128:(kb + 1) * 128], qz[u][:, lo:512],
                                   True, False, [bqz[u]], [pb[bank]], skip_group_check=True)

                            def maskmm(bank, kb, last):
                                if kb >= 8 * c:
                                    j = (kb - 8 * c) // 2
                                    m = (kb - 8 * c) % 2
                                    mm(ps[bank][:, j * 128:(j + 1) * 128], ident[:], sbmask[:, m * 128:(m + 1) * 128],
                                       False, last, [], [pb[bank]], skip_group_check=True)

                            for kb in range(nkb):
                                lo = 128 * max(0, (kb - 8 * c) // 2)
                                bank = bankA[nA % 3]
                                nA += 1
                                zmm(bank, kb, lo)
                                maskmm(bank, kb, True)
                                S.op("act", lambda e, o=sp_all[:, kb, lo:512], i=ps[bank][:, lo:512]: e.activation(out=o, in_=i, func=AF.Softplus),
                                     reads=[pb[bank]], writes=[bsp[kb]])
                            S.op("pool", lambda e: e.memset(lsum[:], 0.0), writes=[blsum])
                            pvb = 4 + h % 2
                            pend = []
                            for kb in range(nkb - 1, -1, -1):
                                lo = 128 * max(0, (kb - 8 * c) // 2)
                                bank = bankB[nB % 3]
                                nB += 1
                                zmm(bank, kb, lo)
                                mm(ps[bank][:, lo:512], negtri[:], sp_all[:, kb, lo:512], False, False, [bsp[kb]], [pb[bank]], skip_group_check=True)
                                if kb < nkb - 1:
                                    mm(ps[bank][:, lo:512], negones[:], lsum[:, lo:512], False, False, [blsum], [pb[bank]], skip_group_check=True)
                                maskmm(bank, kb, True)
                                w = nW % 4
                                nW += 1
                                S.op("act", lambda e, o=wt[w][:, lo:512], i=ps[bank][:, lo:512]: e.activation(out=o, in_=i, func=AF.Exp),
                                     reads=[pb[bank]], writes=[bwt[w]])
                                pend.append((ps[pvb][:, lo:512], va[:, kb, pr * 128:(pr + 1) * 128], wt[w][:, lo:512],
                                             kb == nkb - 1, kb == 0, [bwt[w]], [pb[pvb]]))
                                if len(pend) > 2:
                                    a_ = pend.pop(0)
                                    mm(*a_, skip_group_check=True)
                                if kb > 0:
                                    S.op("dve", lambda e, o=lsum[:, lo:512], s_=sp_all[:, kb, lo:512]: e.tensor_tensor(out=o, in0=o, in1=s_, op=ALU.add),
                                         reads=[bsp[kb], blsum], writes=[blsum])
                            for a_ in pend:
                                mm(*a_, skip_group_check=True)
                            evac(oT[r0:r1, pr, qs:qs + 512], ps[pvb][r0:r1, :], [pb[pvb]], [], eng="dve")
                    S.barrier()
              with _Scope() as st:
                    xres, bxres = load_xres(st, x_own)
                    with _Scope() as st2:
                      if STOP >= 5:
                        outproj_residual(st2, oT, w_out0, xres, bxres, "op0")
                        S.barrier()
                    with _Scope() as st2:
                      if STOP >= 6:
                        ffn_layer(st2, 0, xres, bxres, (hole0, hole1))
                    dxo = S.dma_sem()
                    for blk in range(NB):
                        S.dma("sp", dxo, x1_d[blk], xres[:, blk, :])
                    S.barrier()

        if do0:
            if mode == "R":
                kvd = (nc.dram_tensor("kv_kTa", [128, 4, NBA * 128], BF16, kind="Internal").ap(),
                       nc.dram_tensor("kv_va", [128, NBA, 512], BF16, kind="Internal").ap(),
                       nc.dram_tensor("kv_kz", [128, 4, NBA * 128], BF16, kind="Internal").ap(),
                       nc.dram_tensor("kv_vb", [128, NBA, 2, 72], BF16, kind="Internal").ap())
                layer0(x_own, sbmask_d, swa_bias, swa_mask, x1_d, ("save", kvd))
                layer0(x_oth, sbmask2_d, swa_bias2, swa_mask2, x1_o, ("load", kvd))
            else:
                layer0(x_own, sbmask_d, swa_bias, swa_mask, x1_d)

        if do1:
            def tt(eng, out, in0, in1, op, reads, writes):
                return S.op(eng, lambda e: e.tensor_tensor(out=out, in0=in0, in1=in1, op=op), reads=reads, writes=writes)

            def act(out, in_, func, reads, writes, **kw):
                return S.op("act", lambda e: e.activation(out=out, in_=in_, func=func, **kw), reads=reads, writes=writes)

            def ts(eng, out, in0, s1, s2, op0, op1, reads, writes):
                if op1 is None:
                    return S.op(eng, lambda e: e.tensor_scalar(out=out, in0=in0, scalar1=s1, scalar2=None, op0=op0), reads=reads, writes=writes)
                return S.op(eng, lambda e: e.tensor_scalar(out=out, in0=in0, scalar1=s1, scalar2=s2, op0=op0, op1=op1), reads=reads, writes=writes)

            def cp(eng, out, in_, reads, writes):
                return S.op(eng, lambda e: e.tensor_copy(out=out, in_=in_), reads=reads, writes=writes)

            def tr(out, in_, reads, writes):
                return S.op("pe", lambda e: e.transpose(out, in_, ident[:]), reads=reads, writes=writes)

            SCALE = 96.0 ** -0.5
            with _Scope() as l1:
              hole0 = AR.mark()
              oT1 = sb(l1, "oT1", [128, 8, NB * 128], BF16)
              hole1 = AR.mark()
              with _Scope() as att:
                ckvT = sb(att, "ckvT", [128, 2, NBA * 128], BF16)
                krR = sb(att, "krR", [128, NBA * 128], BF16)
                cqT = sb(att, "cqT", [128, 3, NB * 128], BF16)
                cosq = sb(att, "cosq", [128, NB * 128], F32)
                sinq = sb(att, "sinq", [128, NB * 128], F32)
                rcst = sb(att, "rcst", [128, 8], F32)
                mmask = sb(att, "mmask", [128, 256], BF16)
                S.dma("sp", cd, rcst[:], rope_cst_d[:, :])
                S.dma("pool", cdp, mmask[:], mlamask_d[:, :])
                S.barrier()

                def rope_tables(st, pos_row, ntok, cosT, sinT):
                    pi_ = sb(st, "rt_pi", [128, 1024], I32)
                    pf = sb(st, "rt_pf", [128, 1024], F32)
                    u_ = sb(st, "rt_u", [128, 1024], F32)
                    ki = sb(st, "rt_ki", [128, 1024], I32)
                    kf = sb(st, "rt_kf", [128, 1024], F32)
                    bpi, bpf, bu, bki, bkf = Buf(), Buf(), Buf(), Buf(), Buf()
                    dpi = S.dma_sem()
                    R = slice(64, 96)
                    for c0 in range(0, ntok, 1024):
                        S.dma("sp", dpi, pi_[R, :], pos_row[0:1, c0:c0 + 1024].partition_broadcast(32), writes=[bpi])
                        cp("dve", pf[R, :], pi_[R, :], [bpi], [bpf])
                        for (offcol, dst, is_sin) in ((1, sinT, True), (2, cosT, False)):
                            ts("dve", u_[R, :], pf[R, :], rcst[R, 0:1], rcst[R, offcol:offcol + 1], ALU.mult, ALU.add, [bpf], [bu])
                            cp("dve", ki[R, :], u_[R, :], [bu], [bki])
                            cp("dve", kf[R, :], ki[R, :], [bki], [bkf])
                            tt("dve", u_[R, :], u_[R, :], kf[R, :], ALU.subtract, [bkf, bu], [bu])
                            ts("dve", kf[R, :], u_[R, :], 0.5, None, ALU.is_gt, None, [bu], [bkf])
                            tt("dve", u_[R, :], u_[R, :], kf[R, :], ALU.subtract, [bkf, bu], [bu])
                            if is_sin:
                                act(dst[R, c0:c0 + 1024], u_[R, :], AF.Sin, [bu], [], scale=rcst[R, 3:4])
                            else:
                                act(dst[R, c0:c0 + 1024], u_[R, :], AF.Sin, [bu], [], scale=6.283184)

                rope_units = []
                if mode == "R":
                    rmark = AR.mark()
                    cosk = sb(att, "cosk", [128, NBA * 128], F32)
                    sink = sb(att, "sink", [128, NBA * 128], F32)
                    r_pi = sb(att, "rt_pi", [128, 1024], I32)
                    r_pf = sb(att, "rt_pf", [128, 1024], F32)
                    r_u = sb(att, "rt_u", [128, 1024], F32)
                    r_ki = sb(att, "rt_ki", [128, 1024], I32)
                    r_kf = sb(att, "rt_kf", [128, 1024], F32)
                    rb_pi, rb_pf, rb_u, rb_ki, rb_kf = Buf(), Buf(), Buf(), Buf(), Buf()
                    r_dpi = S.dma_sem()

                    def make_unit(c0, is_sin):
                        def unit():
                            R = slice(64, 96)
                            if is_sin:
                                S.dma("sp", r_dpi, r_pi[R, :], pos_all[0:1, c0:c0 + 1024].partition_broadcast(32), writes=[rb_pi])
                                cp("dve", r_pf[R, :], r_pi[R, :], [rb_pi], [rb_pf])
                            offcol = 1 if is_sin else 2
                            ts("dve", r_u[R, :], r_pf[R, :], rcst[R, 0:1], rcst[R, offcol:offcol + 1], ALU.mult, ALU.add, [rb_pf], [rb_u])
                            cp("dve", r_ki[R, :], r_u[R, :], [rb_u], [rb_ki])
                            cp("dve", r_kf[R, :], r_ki[R, :], [rb_ki], [rb_kf])
                            tt("dve", r_u[R, :], r_u[R, :], r_kf[R, :], ALU.subtract, [rb_kf, rb_u], [rb_u])
                            ts("dve", r_kf[R, :], r_u[R, :], 0.5, None, ALU.is_gt, None, [rb_u], [rb_kf])
                            tt("dve", r_u[R, :], r_u[R, :], r_kf[R, :], ALU.subtract, [rb_kf, rb_u], [rb_u])
                            if is_sin:
                                act(sink[R, c0:c0 + 1024], r_u[R, :], AF.Sin, [rb_u], [], scale=rcst[R, 3:4])
                            else:
                                act(cosk[R, c0:c0 + 1024], r_u[R, :], AF.Sin, [rb_u], [], scale=6.283184)
                        return unit
                    for c0_ in range(0, NBA * 128, 1024):
                        rope_units.append(make_unit(c0_, True))
                        rope_units.append(make_unit(c0_, False))

                def rope_hook():
                    if rope_units:
                        rope_units.pop(0)()

                def down_pass(src_blocks, nblocks, do_q, lat_dst):
                    with _Scope() as st:
                        gbc = load_gbc(st, "gbc_a1", 2)
                        wdn, _b = load_w(st, "wdn", mla_dn, 8, 832)
                        gqb = sb(st, "gqb", [128, 384], F32)
                        S.dma("sp", cd, gqb[:], gq_d[0:1, :].partition_broadcast(128))
                        tp = TokPass(st, "l1p1", gbc, 0, (hole0, hole1))
                        latst = [sb(st, "latst%d" % i, [128, LATW], F32) for i in range(2)]
                        blat = [Buf() for _ in range(2)]
                        dlat = [S.dma_sem() for _ in range(2)]
                        cqn = [sb(st, "cqn%d" % i, [128, 384], BF16) for i in range(2)]
                        bcqn = [Buf() for _ in range(2)]
                        nxq = NormCtx(st, "nxq", 384, 3)
                        S.barrier()
                        n = 0
                        dpend = []
                        for c in range(nblocks // 4):
                            hT, bhT = tp.chunk(src_blocks, c)
                            for j in range(4):
                                blk = 4 * c + j
                                s2 = n % 2
                                A, B_ = 2 + 2 * s2, 3 + 2 * s2
                                n += 1
                                for k in range(8):
                                    mm(ps[A][:], hT[:, k, j * 128:(j + 1) * 128], wdn[:, k, 0:512], k == 0, k == 7, bhT, [pb[A]])
                                for k in range(8):
                                    mm(ps[B_][:, 0:320], hT[:, k, j * 128:(j + 1) * 128], wdn[:, k, 512:832], k == 0, k == 7, bhT, [pb[B_]])
                                if lat_dst is not None:
                                    evac(latst[s2][:, 0:128], ps[A][:, 384:512], [pb[A]], [blat[s2]])
                                    evac(latst[s2][:, 128:LATW], ps[B_][:, 0:320], [pb[B_]], [blat[s2]])
                                    S.dma("sp", dlat[s2], lat_dst[blk], latst[s2][:], reads=[blat[s2]])
                                if do_q:
                                    rmsnorm(nxq, ps[A][:, 0:384], [pb[A]], gqb[:], cqn[s2][:], [bcqn[s2]], 384)

                                    def dp_stage_y(s2=s2, blk=blk):
                                        tb = 6 + s2
                                        psT = ps[tb][:].bitcast(BF16)
                                        for k in range(3):
                                            tr(psT[:, k * 128:(k + 1) * 128], cqn[s2][:, k * 128:(k + 1) * 128], [bcqn[s2]], [pb[tb]])
                                        evac(cqT[:, :, blk * 128:(blk + 1) * 128], psT[:, 0:384].rearrange("p (k t) -> p k t", k=3), [pb[tb]], [])
                                    dpend.append(dp_stage_y)
                                    if len(dpend) > 1:
                                        dpend.pop(0)()
                            rope_hook()
                        for f_ in dpend:
                            f_()
                        S.barrier()

                if mode == "B":
                    down_pass(x1_eo, NBA, False, lat_all)
                    down_pass(x1_d, NB, True, None)
                elif mode == "R":
                    down_pass(x1_d, NB, True, lat_all[0:NB])
                    down_pass(x1_o, NB, False, lat_all[NB:NBA])
                else:
                    down_pass(x1_d, NB, True, lat_own.rearrange("(b p) n -> b p n", p=128))
                    dcc = S.dma_sem()
                    dcc[1] += 16 if not os.environ.get("KNOCC") else 0
                    if not os.environ.get("KNOCC"):
                      S.ops["pool"].append(([], (lambda e: e.collective_compute("AllGather", op=ALU.bypass, replica_groups=CCG,
                                                                             ins=[lat_own[:, :]], outs=[lat_cc[:, :]])), (dcc[0], 16)))
                    S.barrier()

                with _Scope() as st:
                    if mode == "R":
                        while rope_units:
                            rope_hook()
                        S.barrier()
                        cp("pool", cosq[64:96, :], cosk[64:96, 0:NB * 128], [], [])
                        cp("pool", sinq[64:96, :], sink[64:96, 0:NB * 128], [], [])
                        S.barrier()
                    else:
                        cosk = sb(st, "cosk", [128, NBA * 128], F32)
                        sink = sb(st, "sink", [128, NBA * 128], F32)
                        with _Scope() as st2:
                            rope_tables(st2, pos_all, NBA * 128, cosk, sink)
                            rope_tables(st2, pos_own, NB * 128, cosq, sinq)
                            S.barrier()
                    gkvb = sb(st, "gkvb", [128, 256], F32)
                    S.dma("sp", cd, gkvb[:], gkv_d[0:1, :].partition_broadcast(128))
                    latc = [sb(st, "latc%d" % i, [128, LATW], F32) for i in range(3)]
                    blc = [Buf() for _ in range(3)]
                    dlc = [S.dma_sem() for _ in range(3)]
                    ckvn = [sb(st, "ckvn%d" % i, [128, 256], BF16) for i in range(2)]
                    bckvn = [Buf() for _ in range(2)]
                    krb = [sb(st, "krb%d" % i, [128, 256], BF16) for i in range(2)]
                    bkrb = [Buf() for _ in range(2)]
                    t1 = [sb(st, "kt1_%d" % i, [128, 128], F32) for i in range(2)]
                    t2 = [sb(st, "kt2_%d" % i, [128, 128], F32) for i in range(2)]
                    bt1 = [Buf() for _ in range(2)]
                    bt2 = [Buf() for _ in range(2)]
                    nxk = NormCtx(st, "nxk", 256, 3)
                    S.barrier()
                    R = slice(64, 96)
                    def p2_stage_y(blk):
                        s2 = blk % 2
                        cols = slice(blk * 128, (blk + 1) * 128)
                        tb = s2
                        psT = ps[tb][:].bitcast(BF16)
                        tr(psT[:, 0:128], ckvn[s2][:, 0:128], [bckvn[s2]], [pb[tb]])
                        tr(psT[:, 128:256], ckvn[s2][:, 128:256], [bckvn[s2]], [pb[tb]])
                        tr(psT[:, 256:384], krb[s2][:, 0:128], [bkrb[s2]], [pb[tb]])
                        tr(psT[:, 384:512], krb[s2][:, 128:256], [bkrb[s2]], [pb[tb]])
                        evac(ckvT[:, :, cols], psT[:, 0:256].rearrange("p (k t) -> p k t", k=2), [pb[tb]], [])
                        tt("dve", t1[s2][R, :], psT[R, 256:384], cosk[R, cols], ALU.mult, [pb[tb]], [bt1[s2]])
                        tt("dve", t2[s2][R, :], psT[R, 384:512], sink[R, cols], ALU.mult, [pb[tb]], [bt2[s2]])
                        tt("dve", krR[R, cols], t1[s2][R, :], t2[s2][R, :], ALU.add, [bt1[s2], bt2[s2]], [])

                    for blk in range(NBA if STOP1 >= 3 else 0):
                        s3, s2 = blk % 3, blk % 2
                        S.dma("sp", dlc[s3], latc[s3][:], lat_all[blk], writes=[blc[s3]])
                        rmsnorm(nxk, latc[s3][:, 0:256], [blc[s3]], gkvb[:], ckvn[s2][:], [bckvn[s2]], 256)
                        cp("dve", krb[s2][:], latc[s3][:, 192:LATW], [blc[s3]], [bkrb[s2]])
                        if blk >= 1:
                            p2_stage_y(blk - 1)
                    if STOP1 >= 3:
                        p2_stage_y(NBA - 1)
                    S.barrier()

                if mode == "R":
                    AR.release(rmark)
                with _Scope() as st:
                    wuq, _b = load_w(st, "wuq", mla_uq, 3, 1568)
                    wuqs, _b = load_w(st, "wuqs", mla_uqs, 3, 1568)
                    wkn, _b = load_w(st, "wkn", mla_kn, 2, 1024)
                    wv, _b = load_w(st, "wv", mla_v, 2, 1024)
                    vgf = sb(st, "vg", [128, NBA * 4 * 72 + 64], BF16)
                    S.op("pool", lambda e: e.memset(vgf[:], 1.0))
                    vg = vgf[:, 0:NBA * 4 * 72].rearrange("p (b h d) -> p b h d", b=NBA, h=4)
                    kTh = [sb(st, "kTh%d" % i, [128, NBA * 128], BF16) for i in range(2)]
                    qTh = [sb(st, "qTh%d" % i, [128, NB * 128], BF16) for i in range(2)]
                    for i_ in range(2):
                        S.op("pool", lambda e, t_=kTh[i_]: e.memset(t_[:], 0.0))
                        S.op("pool", lambda e, t_=qTh[i_]: e.memset(t_[:], 0.0))
                    bkTh = [Buf() for _ in range(2)]
                    bqTh = [Buf() for _ in range(2)]
                    bvg = Buf()
                    wt = [sb(st, "mwt%d" % i, [128, 512], BF16) for i in range(4)]
                    bwt = [Buf() for _ in range(4)]
                    q1 = [sb(st, "q1_%d" % i, [128, 512], F32) for i in range(2)]
                    q2 = [sb(st, "q2_%d" % i, [128, 512], F32) for i in range(2)]
                    bq1 = [Buf() for _ in range(2)]
                    bq2 = [Buf() for _ in range(2)]
                    rden = sb(st, "m_rden", [128, 512], F32)
                    brden = Buf()
                    ouf = sb(st, "m_ouf", [128, 512], F32)
                    bouf = Buf()
                    otb = [sb(st, "m_otb%d" % i, [128, 512], BF16) for i in range(2)]
                    botb = [Buf() for _ in range(2)]
                    dsh = [S.dma_sem() for _ in range(2)]
                    S.barrier()
                    R = slice(64, 96)
                    npre_, nS_, nW_, nq_, nodd_ = [0], [0], [0], [0], [0]
                    def head_v(h):
                        grp, hh = h // 4, h % 4
                        if True:
                            for blk in range(NBA):
                                bk = 6 + npre_[0] % 2
                                npre_[0] += 1
                                for k in range(2):
                                    mm(ps[bk][:, 0:256], ckvT[:, k, blk * 128:(blk + 1) * 128], wv[:, k, grp * 256:(grp + 1) * 256],
                                       k == 0, k == 1, [], [pb[bk]])
                                evac(vg[:, blk, :, 0:64], ps[bk][:, 0:256].rearrange("p (a d) -> p a d", a=4), [pb[bk]], [bvg])
                    def head_kq(h):
                        grp, hh = h // 4, h % 4
                        hs = h % 2
                        for c8 in range(8):
                            bk = 6 + npre_[0] % 2
                            npre_[0] += 1
                            for k in range(2):
                                mm(ps[bk][0:64, :], wkn[:, k, h * 64:(h + 1) * 64], ckvT[:, k, c8 * 512:(c8 + 1) * 512], k == 0, k == 1, [], [pb[bk]])
                            evac(kTh[hs][0:64, c8 * 512:(c8 + 1) * 512], ps[bk][0:64, :], [pb[bk]], [bkTh[hs]])
                        cp("dve", kTh[hs][R, :], krR[R, :], [], [bkTh[hs]])
                        for c in range(4):
                            cols = slice(c * 512, (c + 1) * 512)
                            bM, bS = 6, 7
                            for k in range(3):
                                mm(ps[bM][:], wuq[:, k, h * 96:h * 96 + 128], cqT[:, k, cols], k == 0, k == 2, [], [pb[bM]])
                            for k in range(3):
                                mm(ps[bS][:], wuqs[:, k, h * 96:h * 96 + 128], cqT[:, k, cols], k == 0, k == 2, [], [pb[bS]])
                            evac(qTh[hs][0:64, cols], ps[bM][0:64, :], [pb[bM]], [bqTh[hs]])
                            s2 = nq_[0] % 2
                            nq_[0] += 1
                            tt("dve", q1[s2][R, :], ps[bM][R, :], cosq[R, cols], ALU.mult, [pb[bM]], [bq1[s2]])
                            tt("dve", q2[s2][R, :], ps[bS][R, :], sinq[R, cols], ALU.mult, [pb[bS]], [bq2[s2]])
                            tt("dve", qTh[hs][R, cols], q1[s2][R, :], q2[s2][R, :], ALU.add, [bq1[s2], bq2[s2]], [bqTh[hs]])
                    def head_att(h):
                        grp, hh = h // 4, h % 4
                        hs = h % 2
                        for c in range(4):
                            qs = c * 512
                            blocks = [(0, e_) for e_ in range(4 * c + 4)] + [(1, o_) for o_ in range(4 * c + 4)]
                            pvb = 3 + (4 * h + c) % 2
                            pend = []
                            for bi, (par, idx) in enumerate(blocks):
                                kb = idx if par == 0 else NB + idx
                                j0 = max(0, idx - 4 * c)
                                lo = 128 * j0
                                bank = nS_[0] % 3
                                nS_[0] += 1
                                diag = idx >= 4 * c
                                mm(ps[bank][:, lo:512], kTh[hs][:, kb * 128:(kb + 1) * 128], qTh[hs][:, qs + lo:qs + 512],
                                   True, not diag, [bkTh[hs], bqTh[hs]], [pb[bank]], skip_group_check=True)
                                if diag:
                                    mm(ps[bank][:, lo:lo + 128], ident[:], mmask[:, par * 128:(par + 1) * 128], False, True, [], [pb[bank]], skip_group_check=True)
                                w = nW_[0] % 4
                                nW_[0] += 1
                                act(wt[w][:, lo:512], ps[bank][:, lo:512], AF.Exp, [pb[bank]], [bwt[w]], scale=SCALE)
                                vo = (kb * 4 + hh) * 72
                                pend.append((ps[pvb][:, lo:512], vgf[:, vo:vo + 128], wt[w][:, lo:512], bi == 0, bi == len(blocks) - 1,
                                             [bwt[w], bvg], [pb[pvb]]))
                                if len(pend) > 2:
                                    a_ = pend.pop(0)
                                    mm(*a_, skip_group_check=True)
                            for a_ in pend:
                                mm(*a_, skip_group_check=True)
                            act(rden[64:65, :], ps[pvb][64:65, :], AF.Ln, [pb[pvb]], [brden])
                            act(rden[64:65, :], rden[64:65, :], AF.Exp, [brden], [brden], scale=-1.0)
                            mm(ps[5][0:64, :], onesf[64:65, 0:64], rden[64:65, :], True, True, [brden], [pb[5]])
                            act(ouf[0:64, :], ps[pvb][0:64, :], AF.Copy, [pb[pvb]], [bouf])
                            if h % 2 == 0:
                                tt("dve", oT1[0:64, h // 2, qs:qs + 512], ouf[0:64, :], ps[5][0:64, :], ALU.mult, [bouf, pb[5]], [])
                            else:
                                so = nodd_[0] % 2
                                nodd_[0] += 1
                                tt("dve", otb[so][0:64, :], ouf[0:64, :], ps[5][0:64, :], ALU.mult, [bouf, pb[5]], [botb[so]])
                                S.dma("sp", dsh[so], oT1[64:128, h // 2, qs:qs + 512], otb[so][0:64, :], reads=[botb[so]])

                    for h in range(16 if STOP1 >= 4 else 0):
                        if h % 4 == 0:
                            head_v(h)
                        if h == 0:
                            head_kq(0)
                        if h + 1 < 16:
                            head_kq(h + 1)
                        head_att(h)
                    S.barrier()
              with _Scope() as st:
                    xres, bxres = load_xres(st, x1_d)
                    with _Scope() as st2:
                        outproj_residual(st2, oT1, mla_wo, xres, bxres, "op1")
                        S.barrier()
                    with _Scope() as st2:
                        ffn_layer(st2, 1, xres, bxres, (hole0, hole1))
                    with _Scope() as st2:
                        gfin = load_gbc(st2, "gfin", 4)
                        S.barrier()
                        ost = [sb(st2, "ost%d" % i, [128, 1024], F32) for i in range(2)]
                        bost = [Buf() for _ in range(2)]
                        dost = [S.dma_sem() for _ in range(2)]
                        nxf = NormCtx(st2, "nxf", 1024, 3)
                        for blk in range(NB):
                            s2 = blk % 2
                            rmsnorm(nxf, xres[:, blk, :], [bxres[blk]], gfin[:], ost[s2][:], [bost[s2]], 1024)
                            S.dma("sp", dost[s2], out_own[blk], ost[s2][:], reads=[bost[s2]])
                        S.barrier()

        S.barrier()
        S.flush(block)
    return nc


def _t5_bucket(rel):
    rel = np.maximum(rel, 0)
    relf = np.maximum(rel, 1).astype(np.float32)
    large = 16 + (np.log(relf / np.float32(16)) / np.float32(math.log(128 / 16)) * np.float32(16)).astype(np.int32)
    large = np.minimum(large, 31)
    return np.where(rel < 16, rel, large)


def _consts(p, rel_bias_table):
    s = np.arange(128)[:, None]
    t = np.arange(128)[None, :]
    c = {}
    c["ident"] = np.eye(128, dtype=np.float32)
    c["negtri"] = np.where(s >= t, -1.0, 0.0).astype(np.float32)
    full = np.zeros((128, 128), np.float32)
    none = np.full((128, 128), NEG, np.float32)
    strict = np.where(s < t, 0.0, NEG).astype(np.float32)
    incl = np.where(s <= t, 0.0, NEG).astype(np.float32)
    c["sbmask"] = np.concatenate([strict, none] if p == 0 else [full, strict], axis=1)
    c["mlamask"] = np.concatenate([incl, none] if p == 0 else [full, incl], axis=1)
    bias = np.zeros((128, 2, 3, 4, 128), np.float32)
    mask = np.zeros((128, 2, 3, 4, 128), np.float32)
    for r in range(3):
        rel = (p + 1 - r) * 128 + t - s
        valid = (rel >= 0) & (rel < 128)
        bk = _t5_bucket(rel)
        for g in range(2):
            for q in range(4):
                hd = 4 * g + (0, 2, 1, 3)[q]
                gathered = rel_bias_table[bk, hd]
                bias[:, g, r, q, :] = np.where(valid, gathered, np.float32(0.0))
                mask[:, g, r, q, :] = np.where(valid, 0.0, NEG)
    c["swa_bias"] = bias.reshape(128, 6 * 512)
    c["swa_mask"] = mask.reshape(128, 6 * 512)
    return c


def _prep_common(inp):
    w_in = inp["even_w_in"][0]
    kb0, kb1 = w_in[:, 2048:2112], w_in[:, 2112:2176]
    d = {}
    z64 = np.zeros_like(kb0)
    d["w_kv0"] = np.ascontiguousarray(np.concatenate([w_in[:, 512:1024], w_in[:, 1024:1536], kb0, z64, z64, kb0, kb1, z64, z64, kb1,
                                                      w_in[:, 2176:2304]], axis=1))
    d["w_q0"] = np.ascontiguousarray(np.concatenate([w_in[:, 0:512], w_in[:, 1536:2048]], axis=1))
    d["w_out0"] = np.ascontiguousarray(inp["even_w_out"][0])
    d["gvec"] = np.ascontiguousarray(np.stack([inp["attn_norm"][0], inp["ffn_norm"][0], inp["attn_norm"][1], inp["ffn_norm"][1], inp["final_norm"]]))
    d["ffn_g"] = np.ascontiguousarray(inp["ffn_w_gate"])
    d["ffn_u"] = np.ascontiguousarray(inp["ffn_w_up"])
    d["ffn_d"] = np.ascontiguousarray(inp["ffn_w_down"])
    d["sinks"] = np.ascontiguousarray(inp["even_sinks"][0:1])
    return d


def _prep_l1(inp):
    d = {}
    wd = inp["mla_w_down"][0]
    z32 = np.zeros((1024, 32), np.float32)
    z64 = np.zeros((1024, 64), np.float32)
    d["mla_dn"] = np.ascontiguousarray(np.concatenate([wd[:, 0:384], wd[:, 384:640], wd[:, 640:672], z32, z64,
                                                       wd[:, 656:672], wd[:, 640:656], z32], axis=1))
    uq = inp["mla_w_uq"][0].reshape(384, 16, 96)
    uqs = np.concatenate([uq[:, :, 0:64], uq[:, :, 80:96], uq[:, :, 64:80]], axis=2)
    zp = np.zeros((384, 32), np.float32)
    d["mla_uq"] = np.ascontiguousarray(np.concatenate([uq.reshape(384, 1536), zp], axis=1))
    d["mla_uqs"] = np.ascontiguousarray(np.concatenate([uqs.reshape(384, 1536), zp], axis=1))
    ukv = inp["mla_w_ukv"][0].reshape(256, 16, 128)
    d["mla_kn"] = np.ascontiguousarray(ukv[:, :, 0:64].reshape(256, 1024))
    d["mla_v"] = np.ascontiguousarray(ukv[:, :, 64:128].reshape(256, 1024))
    d["mla_wo"] = np.ascontiguousarray(inp["mla_w_o"][0])
    d["gq"] = np.ascontiguousarray(inp["mla_q_norm"][0:1])
    d["gkv"] = np.ascontiguousarray(inp["mla_kv_norm"][0:1])
    cst = np.zeros((128, 8), np.float32)
    freqs = np.float32(10000.0) ** (-(np.arange(16, dtype=np.float32) / np.float32(16)))
    for r in range(32):
        sgn = -1.0 if r < 16 else 1.0
        cst[64 + r] = [freqs[r % 16] / np.float32(2.0 * PI), 0.0, 0.25, 6.283184 * sgn, 0.0, 0.0, 0.0, 0.0]
    d["rope_cst"] = cst
    return d


B_KEYS = ("gvec", "ident", "ffn_g", "ffn_u", "ffn_d", "mla_dn", "mla_uq", "mla_uqs", "mla_kn", "mla_v", "mla_wo", "gq", "gkv",
          "mlamask", "rope_cst", "pos_own", "pos_all", "x1_own", "x1_eo")


def run_layer1(inputs, x1, cores=tuple(range(8))):
    inp = {k: np.asarray(v) for k, v in inputs.items()}
    common = _prep_common(inp)
    common.update(_prep_l1(inp))
    pos = np.asarray(inputs["positions"]).astype(np.int32)
    in_maps = []
    for core in cores:
        b, p = core // 2, core % 2
        m = dict(common)
        m.update(_consts(p, np.asarray(inputs["rel_bias_table"], np.float32)))
        pb_ = pos[b].reshape(NB, 2, 128)
        m["pos_own"] = np.ascontiguousarray(pb_[:, p].reshape(1, NB * 128))
        m["pos_all"] = np.ascontiguousarray(pb_.transpose(1, 0, 2).reshape(1, NBA * 128))
        m["x1_own"] = np.ascontiguousarray(x1[core])
        m["x1_eo"] = np.ascontiguousarray(np.concatenate([x1[2 * b], x1[2 * b + 1]], axis=0))
        in_maps.append({k: m[k] for k in B_KEYS})
    res = run_bass_kernel_spmd(_get_nc("B"), in_maps, core_ids=list(range(len(cores))))
    return {core: r["out_own"] for core, r in zip(cores, res.results)}


A_KEYS = ("gvec", "ident", "ffn_g", "ffn_u", "ffn_d", "x_own", "x_all", "w_kv0", "w_q0", "w_out0", "negtri", "sbmask",
          "swa_bias", "swa_mask", "sinks")

_CACHE = {}


def _get_nc(mode):
    if mode not in _CACHE:
        _CACHE[mode] = build(mode)
    return _CACHE[mode]


def run_layer0(inputs):
    x = np.asarray(inputs["x"], np.float32)
    common = _prep_common({k: np.asarray(v) for k, v in inputs.items()})
    in_maps = []
    for core in range(8):
        b, p = core // 2, core % 2
        xb = x[b].reshape(NB, 2, 128, 1024)
        m = dict(common)
        m.update(_consts(p, np.asarray(inputs["rel_bias_table"], np.float32)))
        m["x_own"] = np.ascontiguousarray(xb[:, p])
        m["x_all"] = np.ascontiguousarray(x[b].reshape(NBA, 128, 1024))
        in_maps.append({k: m[k] for k in A_KEYS})
    res = run_bass_kernel_spmd(_get_nc("A"), in_maps, core_ids=list(range(8)))
    return [r["x1_own"] for r in res.results]


F_KEYS = tuple(dict.fromkeys(A_KEYS + tuple(k for k in B_KEYS if k not in ("x1_own", "x1_eo"))))


def run_fused(inputs):
    inp = {k: np.asarray(v) for k, v in inputs.items()}
    x = np.asarray(inputs["x"], np.float32)
    common = _prep_common(inp)
    common.update(_prep_l1(inp))
    pos = np.asarray(inputs["positions"]).astype(np.int32)
    in_maps = []
    for core in range(8):
        b, p = core // 2, core % 2
        m = dict(common)
        m.update(_consts(p, np.asarray(inputs["rel_bias_table"], np.float32)))
        xb = x[b].reshape(NB, 2, 128, 1024)
        m["x_own"] = np.ascontiguousarray(xb[:, p])
        m["x_all"] = np.ascontiguousarray(x[b].reshape(NBA, 128, 1024))
        pb_ = pos[b].reshape(NB, 2, 128)
        m["pos_own"] = np.ascontiguousarray(pb_[:, p].reshape(1, NB * 128))
        m["pos_all"] = np.ascontiguousarray(pb_.transpose(1, 0, 2).reshape(1, NBA * 128))
        in_maps.append({k: m[k] for k in F_KEYS})
    res = run_bass_kernel_spmd(_get_nc("F"), in_maps, core_ids=list(range(8)))
    return {core: r["out_own"] for core, r in enumerate(res.results)}


R_KEYS = F_KEYS + ("x_oth", "sbmask2", "swa_bias2", "swa_mask2")


def run_redundant(inputs):
    inp = {k: np.asarray(v) for k, v in inputs.items()}
    x = np.asarray(inputs["x"], np.float32)
    common = _prep_common(inp)
    common.update(_prep_l1(inp))
    pos = np.asarray(inputs["positions"]).astype(np.int32)
    rbt = np.asarray(inputs["rel_bias_table"], np.float32)
    s_ = np.arange(128)[:, None]
    t_ = np.arange(128)[None, :]
    incl = np.where(s_ <= t_, 0.0, NEG).astype(np.float32)
    in_maps = []
    for core in range(8):
        b, p = core // 2, core % 2
        m = dict(common)
        m.update(_consts(p, rbt))
        c2 = _consts(1 - p, rbt)
        m["sbmask2"], m["swa_bias2"], m["swa_mask2"] = c2["sbmask"], c2["swa_bias"], c2["swa_mask"]
        other = np.full((128, 128), NEG, np.float32) if p == 0 else np.zeros((128, 128), np.float32)
        m["mlamask"] = np.concatenate([incl, other], axis=1)
        xb = x[b].reshape(NB, 2, 128, 1024)
        m["x_own"] = np.ascontiguousarray(xb[:, p])
        m["x_oth"] = np.ascontiguousarray(xb[:, 1 - p])
        m["x_all"] = np.ascontiguousarray(x[b].reshape(NBA, 128, 1024))
        pb_ = pos[b].reshape(NB, 2, 128)
        m["pos_own"] = np.ascontiguousarray(pb_[:, p].reshape(1, NB * 128))
        m["pos_all"] = np.ascontiguousarray(np.concatenate([pb_[:, p].reshape(-1), pb_[:, 1 - p].reshape(-1)]).reshape(1, NBA * 128))
        in_maps.append({k: m[k] for k in R_KEYS})
    res = run_bass_kernel_spmd(_get_nc("R"), in_maps, core_ids=list(range(8)))
    return {core: r["out_own"] for core, r in enumerate(res.results)}


FUSED = True


def kernel(**inputs):
    if FUSED:
        outs = run_redundant(inputs)
    else:
        x1l = run_layer0(inputs)
        outs = run_layer1(inputs, {c: x1l[c] for c in range(8)})
    out = np.zeros((4, 4096, 1024), np.float32)
    for core in range(8):
        b, p = core // 2, core % 2
        out[b].reshape(NB, 2, 128, 1024)[:, p] = outs[core]
    return out
```

```python
from contextlib import ExitStack
import math
import numpy as np
import concourse.bass as bass
import concourse.mybir as mybir
from concourse.bass_utils import run_bass_kernel_spmd

F32 = mybir.dt.float32
BF16 = mybir.dt.bfloat16
I32 = mybir.dt.int32
AF = mybir.ActivationFunctionType
ALU = mybir.AluOpType

NEG = -30000.0
NB = 16
NBA = 32
FF = 2816
NFB = FF // 128
ARENA_F32 = 52800
LATW = 448
import os
STOP = int(os.environ.get('KSTOP', '99'))
STOP1 = int(os.environ.get('KSTOP1', '99'))
SUB = int(os.environ.get('KSUB', '99'))
PI = math.pi


class Buf:
    __slots__ = ("w", "r", "excl")

    def __init__(self, excl=False):
        self.w = None
        self.r = []
        self.excl = excl


class Sched:
    ENG = ("pe", "act", "dve", "pool", "sp")

    def __init__(self, nc, stack):
        self.nc = nc
        self.stack = stack
        self.sem = {e: stack.enter_context(nc.semaphore("sem_" + e)) for e in self.ENG}
        self.cnt = {e: 0 for e in self.ENG}
        self.ops = {e: [] for e in self.ENG}
        self.waited = {e: {} for e in self.ENG}
        self.dsems = []
        self.lazy = set()

    def dma_sem(self):
        s = self.stack.enter_context(self.nc.semaphore("dsem%d" % len(self.dsems)))
        d = [s, 0]
        self.dsems.append(d)
        return d

    def _waits(self, eng, deps):
        out = []
        for ev in deps:
            if ev is None:
                continue
            key, sem, val, src = ev
            if src == "pe" and eng == "pe":
                continue
            if self.waited[eng].get(key, 0) >= val:
                continue
            self.waited[eng][key] = val
            out.append((sem, val))
        return out

    def _deps(self, reads, writes, extra, eng=None):
        deps = list(extra)
        for b in reads:
            deps.append(b.w)
            if b.excl:
                deps.extend(r for r in b.r if r[3] != eng)
        for b in writes:
            deps.append(b.w)
            deps.extend(b.r)
        return deps

    def _mark(self, ev, reads, writes):
        for b in reads:
            b.r.append(ev)
        for b in writes:
            b.w = ev
            b.r = []

    def op(self, eng, fn, reads=(), writes=(), extra=()):
        waits = self._waits(eng, self._deps(reads, writes, extra, eng))
        self.cnt[eng] += 1
        ev = (eng, self.sem[eng], self.cnt[eng], eng)
        self.ops[eng].append((waits, fn, (self.sem[eng], 1)))
        self._mark(ev, reads, writes)
        return ev

    def dma(self, q, dsem, out, in_, reads=(), writes=(), extra=()):
        waits = self._waits(q, self._deps(reads, writes, extra, q))
        dsem[1] += 16
        ev = (id(dsem), dsem[0], dsem[1], "dma")
        self.ops[q].append((waits, (lambda e, o=out, i=in_: e.dma_start(out=o, in_=i)), (dsem[0], 16)))
        self._mark(ev, reads, writes)
        return ev

    def wait_on(self, eng, evs):
        waits = self._waits(eng, evs)
        if waits:
            self.ops[eng].append((waits, None, None))

    def barrier(self):
        evs = [(e, self.sem[e], self.cnt[e], e) for e in self.ENG if self.cnt[e] > 0]
        evs += [(id(d), d[0], d[1], "dma") for d in self.dsems if d[1] > 0 and id(d) not in self.lazy]
        for e in self.ENG:
            self.wait_on(e, [ev for ev in evs if not (ev[3] == "pe" and e == "pe")])

    def flush(self, block):
        def mk(e):
            def run(engine):
                for waits, fn, inc in self.ops[e]:
                    for s, v in waits:
                        engine.wait_ge(s, v)
                    if fn is not None:
                        fn(engine).then_inc(inc[0], inc[1])
            return run
        block.tensor(mk("pe"))
        block.scalar(mk("act"))
        block.vector(mk("dve"))
        block.gpsimd(mk("pool"))
        block.sync(mk("sp"))


def fgroups():
    out = []
    fb = 0
    while fb < NFB:
        n = min(4, NFB - fb)
        out.append((fb, n))
        fb += n
    return out


def build(mode):
    nc = bass.Bass("TRN2", target_bir_lowering=False)
    do0 = mode in ("A", "F", "R")
    do1 = mode in ("B", "F", "R")

    def din(name, shape, dt=F32):
        return nc.dram_tensor(name, list(shape), dt, kind="ExternalInput").ap()

    def dout(name, shape, dt=F32):
        return nc.dram_tensor(name, list(shape), dt, kind="ExternalOutput").ap()

    def dint(name, shape, dt=F32):
        return nc.dram_tensor(name, list(shape), dt, kind="Internal").ap()

    gvec = din("gvec", [5, 1024])
    ident_d = din("ident", [128, 128])
    ffn_g = din("ffn_g", [2, 1024, FF])
    ffn_u = din("ffn_u", [2, 1024, FF])
    ffn_d = din("ffn_d", [2, FF, 1024])
    if do0:
        x_own = din("x_own", [NB, 128, 1024])
        x_all = din("x_all", [NBA, 128, 1024])
        w_kv0 = din("w_kv0", [1024, 1664])
        w_q0 = din("w_q0", [1024, 1024])
        w_out0 = din("w_out0", [1024, 1024])
        tri_d = din("negtri", [128, 128])
        sbmask_d = din("sbmask", [128, 256])
        swa_bias = din("swa_bias", [128, 6 * 512])
        swa_mask = din("swa_mask", [128, 6 * 512])
        sinks_d = din("sinks", [1, 8])
    if do1:
        mla_dn = din("mla_dn", [1024, 832])
        mla_uq = din("mla_uq", [384, 1568])
        mla_uqs = din("mla_uqs", [384, 1568])
        mla_kn = din("mla_kn", [256, 1024])
        mla_v = din("mla_v", [256, 1024])
        mla_wo = din("mla_wo", [1024, 1024])
        gq_d = din("gq", [1, 384])
        gkv_d = din("gkv", [1, 256])
        mlamask_d = din("mlamask", [128, 256])
        rope_cst_d = din("rope_cst", [128, 8])
        pos_own = din("pos_own", [1, NB * 128], I32)
        pos_all = din("pos_all", [1, NBA * 128], I32)
        out_own = dout("out_own", [NB, 128, 1024])
    if mode == "R":
        x_oth = din("x_oth", [NB, 128, 1024])
        sbmask2_d = din("sbmask2", [128, 256])
        swa_bias2 = din("swa_bias2", [128, 6 * 512])
        swa_mask2 = din("swa_mask2", [128, 6 * 512])
        x1_d = dint("x1_own", [NB, 128, 1024])
        x1_o = dint("x1_oth", [NB, 128, 1024])
        lat_all = dint("lat_all", [NBA, 128, LATW])
    elif mode == "A":
        x1_d = dout("x1_own", [NB, 128, 1024])
    elif mode == "B":
        x1_d = din("x1_own", [NB, 128, 1024])
        x1_eo = din("x1_eo", [NBA, 128, 1024])
        lat_all = dint("lat_all", [NBA, 128, LATW])
    else:
        x1_d = dint("x1_own", [NB, 128, 1024])
        lat_own = dint("lat_own", [NB * 128, LATW])
        if os.environ.get("KCC") == "8":
            CCG = [[0, 1, 2, 3, 4, 5, 6, 7]]
            lat_cc = dint("lat_cc", [8 * NB * 128, LATW])
            lat_all2 = lat_cc[0:NBA * 128, :]
        elif os.environ.get("KCC") == "4":
            CCG = [[0, 1, 2, 3], [4, 5, 6, 7]]
            lat_cc = dint("lat_cc", [4 * NB * 128, LATW])
            lat_all2 = lat_cc[0:NBA * 128, :]
        else:
            CCG = [[0, 1], [2, 3], [4, 5], [6, 7]]
            lat_cc = dint("lat_all", [NBA * 128, LATW])
            lat_all2 = lat_cc
        lat_all = lat_all2.rearrange("(b p) n -> b p n", p=128)

    with ExitStack() as top:
        S = Sched(nc, top)

        arena = top.enter_context(nc.sbuf_tensor("arena", [128, ARENA_F32], F32))

        class Arena:
            def __init__(self):
                self.off = 0
                self.peak = 0

            def mark(self):
                return self.off

            def release(self, m):
                self.off = m

            def alloc(self, shape, dt):
                esz = 4 if dt in (F32, I32) else 2
                n = 1
                for d_ in shape[1:]:
                    n *= d_
                nbytes = (n * esz + 63) // 64 * 64
                o = self.off
                self.off += nbytes
                self.peak = max(self.peak, self.off)
                assert self.off <= ARENA_F32 * 4, ("SBUF arena overflow", self.off)
                v = arena[:, o // 4:(o + nbytes) // 4]
                if dt != F32:
                    v = v.bitcast(dt)
                v = v[0:shape[0], 0:n]
                if len(shape) == 3:
                    v = v.rearrange("p (a b) -> p a b", a=shape[1])
                elif len(shape) == 4:
                    v = v.rearrange("p (a b c) -> p a b c", a=shape[1], b=shape[2])
                return v

        AR = Arena()

        class _Region:
            def __init__(self, lo, hi):
                self.lo, self.hi = lo, hi

            def __enter__(self):
                self.saved = AR.off
                AR.off = self.lo
                return self

            def __exit__(self, *a):
                assert AR.off <= self.hi, ("region overflow", AR.off, self.hi)
                AR.off = self.saved
                return False

        class _Scope:
            def __enter__(self):
                self.m = AR.mark()
                return self

            def __exit__(self, *a):
                AR.release(self.m)
                return False

        def sb(st, name, shape, dt):
            return AR.alloc(list(shape), dt)

        ps = [top.enter_context(nc.psum_tensor("ps%d" % i, [128, 512], F32)) for i in range(8)]
        pb = [Buf(excl=True) for _ in range(8)]
        block = top.enter_context(nc.Block())

        ident = sb(None, "ident", [128, 128], BF16)
        onesf = sb(None, "onesf", [128, 64], F32)
        cd = S.dma_sem()
        cdp = S.dma_sem()
        S.dma("pool", cdp, ident[:], ident_d[:, :])
        S.op("pool", lambda e: e.memset(onesf[:], 1.0))
        S.barrier()

        class NormCtx:
            def __init__(self, st, tag, n, nslots=2):
                self.n = n
                self.junk = sb(st, tag + "_junk", [128, n], BF16)
                self.bjunk = Buf()
                self.ss = [sb(st, "%s_ss%d" % (tag, i), [128, 2], F32) for i in range(nslots)]
                self.bss = [Buf() for _ in range(nslots)]
                self.i = 0

        def rmsnorm(nx, src_ap, src_bufs, g_ap, out_ap, out_bufs, dim, eng_scale="dve"):
            k = nx.i % len(nx.ss)
            nx.i += 1
            ss, bss = nx.ss[k], nx.bss[k]
            S.op("act", lambda e: e.activation(out=nx.junk[:, 0:dim], in_=src_ap, func=AF.Square, accum_out=ss[:, 0:1]),
                 reads=src_bufs, writes=[nx.bjunk, bss])
            S.op("act", lambda e: e.activation(out=ss[:, 1:2], in_=ss[:, 0:1], func=AF.Ln, scale=1.0 / dim, bias=1e-6),
                 reads=[], writes=[bss])
            S.op("act", lambda e: e.activation(out=ss[:, 1:2], in_=ss[:, 1:2], func=AF.Exp, scale=-0.5),
                 reads=[], writes=[bss])
            return S.op(eng_scale, lambda e: e.scalar_tensor_tensor(out=out_ap, in0=src_ap, scalar=ss[:, 1:2], in1=g_ap,
                                                                    op0=ALU.mult, op1=ALU.mult),
                        reads=list(src_bufs) + [bss], writes=out_bufs)

        evac_toggle = [0]

        def evac(out_ap, in_ap, reads, writes, scale=None, eng=None):
            if eng is None:
                eng = ("dve", "act")[evac_toggle[0] % 2]
                evac_toggle[0] += 1
            if eng == "act":
                if scale is None:
                    return S.op("act", lambda e: e.activation(out=out_ap, in_=in_ap, func=AF.Copy), reads=reads, writes=writes)
                return S.op("act", lambda e: e.activation(out=out_ap, in_=in_ap, func=AF.Copy, scale=float(scale)), reads=reads, writes=writes)
            if scale is None:
                return S.op(eng, lambda e: e.tensor_copy(out=out_ap, in_=in_ap), reads=reads, writes=writes)
            return S.op(eng, lambda e: e.tensor_scalar(out=out_ap, in0=in_ap, scalar1=float(scale), scalar2=None, op0=ALU.mult),
                        reads=reads, writes=writes)

        def load_gbc(st, name, row):
            t = sb(st, name, [128, 1024], F32)
            S.dma("sp", cd, t[:], gvec[row:row + 1, :].partition_broadcast(128))
            return t

        def mm(out, lhsT, rhs, start, stop, reads, writes, **kw):
            return S.op("pe", lambda e: e.matmul(out, lhsT, rhs, start=start, stop=stop, **kw), reads=reads, writes=writes)

        class TokPass:
            def __init__(self, st, tag, gbc, bank0, hole=None):
                self.gbc = gbc
                if hole is not None:
                    with _Region(hole[0], hole[1]):
                        self.xst = [sb(st, "%s_x%d" % (tag, i), [128, 1024], F32) for i in range(3)]
                        self.hT = [sb(st, "%s_hT%d" % (tag, i), [128, 8, 512], BF16) for i in range(2)]
                        self.hb = [sb(st, "%s_hb%d" % (tag, i), [128, 1024], BF16) for i in range(2)]
                else:
                    self.hb = [sb(st, "%s_hb%d" % (tag, i), [128, 1024], BF16) for i in range(2)]
                    self.xst = [sb(st, "%s_x%d" % (tag, i), [128, 1024], F32) for i in range(3)]
                    self.hT = [sb(st, "%s_hT%d" % (tag, i), [128, 8, 512], BF16) for i in range(2)]
                self.bx = [Buf() for _ in range(3)]
                self.dx = [S.dma_sem() for _ in range(3)]
                self.bhb = [Buf() for _ in range(2)]
                self.bhT = [[Buf() for _ in range(4)] for _ in range(2)]
                self.nx = NormCtx(st, tag + "_n", 1024, 3)
                self.bank0 = bank0
                self.nblk = 0
                self.nchunk = 0

            def stage1(self, x_sb_ap, x_bufs):
                i = self.nblk
                self.nblk += 1
                hb, bhb = self.hb[i % 2], self.bhb[i % 2]
                rmsnorm(self.nx, x_sb_ap, x_bufs, self.gbc[:], hb[:], [bhb], 1024)
                return i

            def stage2(self, i, j, cslot):
                hb, bhb = self.hb[i % 2], self.bhb[i % 2]
                bk = self.bank0 + (i % 2)
                psT = ps[bk][:].bitcast(BF16)
                for k in range(8):
                    S.op("pe", lambda e, k=k: e.transpose(psT[:, k * 128:(k + 1) * 128], hb[:, k * 128:(k + 1) * 128], ident[:]),
                         reads=[bhb], writes=[pb[bk]])
                evac(self.hT[cslot][:, :, j * 128:(j + 1) * 128], psT.rearrange("p (k t) -> p k t", k=8),
                     [pb[bk]], [self.bhT[cslot][j]])

            def chunk(self, src_blocks_dram, c, nblk=4):
                cslot = self.nchunk % 2
                self.nchunk += 1
                pend = []
                for j in range(nblk):
                    sl = self.nblk % 3
                    S.dma("sp", self.dx[sl], self.xst[sl][:], src_blocks_dram[4 * c + j], writes=[self.bx[sl]])
                    i = self.stage1(self.xst[sl][:], [self.bx[sl]])
                    pend.append((i, j))
                    if len(pend) > 1:
                        self.stage2(*pend.pop(0), cslot)
                for p_ in pend:
                    self.stage2(*p_, cslot)
                return self.hT[cslot], self.bhT[cslot]

        def load_w(st, name, dram2d, kchunks, ncols, c0=0, dsem=None):
            t = sb(st, name, [128, kchunks, ncols], BF16)
            v = dram2d.rearrange("(k p) n -> p k n", p=128)
            b = Buf()
            ev = None
            for k0 in range(0, kchunks, 4):
                k1 = min(kchunks, k0 + 4)
                ev = S.dma("pool", dsem or cdp, t[:, k0:k1, :], v[:, k0:k1, c0:c0 + ncols])
            b.w = ev
            return t, b

        def ffn_layer(st, l, xres, bxres, hole):
            with _Region(hole[0], hole[1]):
                hT = sb(st, "ffn_hT", [128, 8, 512], BF16)
                aT = sb(st, "ffn_aT", [128, NFB, 512], BF16)
            gbc = load_gbc(st, "ffn_gbc", 1 + 2 * l)
            wd = sb(st, "ffn_wd", [128, NFB, 1024], BF16)
            bwd = Buf()
            dwd = S.dma_sem()
            wdview = ffn_d[l].rearrange("(k p) n -> p k n", p=128)
            gview = ffn_g[l].rearrange("(k p) n -> p k n", p=128)
            uview = ffn_u[l].rearrange("(k p) n -> p k n", p=128)
            wg = [sb(st, "ffn_wg%d" % i, [128, 8, 512], BF16) for i in range(2)]
            wu = [sb(st, "ffn_wu%d" % i, [128, 8, 512], BF16) for i in range(2)]
            bwg = [Buf() for _ in range(2)]
            bwu = [Buf() for _ in range(2)]
            dwg = [S.dma_sem() for _ in range(2)]
            dwu = [S.dma_sem() for _ in range(2)]
            hb = [sb(st, "ffn_hb%d" % i, [128, 1024], BF16) for i in range(2)]
            bhb = [Buf() for _ in range(2)]
            bhT = [Buf() for _ in range(4)]
            baT = [Buf() for _ in range(NFB)]
            sg = [sb(st, "ffn_sg%d" % i, [128, 512], F32) for i in range(2)]
            bsg = [Buf() for _ in range(2)]
            nx = NormCtx(st, "ffn_n", 1024, 3)
            S.barrier()
            nstream = 0
            nfb = 0
            nblk = 0
            ndown = 0
            for c in range(4):
                def ffn_stage2(j, k2):
                    bk = k2
                    psT = ps[bk][:].bitcast(BF16)
                    for k in range(8):
                        S.op("pe", lambda e, k=k, psT=psT, h=hb[k2]: e.transpose(psT[:, k * 128:(k + 1) * 128], h[:, k * 128:(k + 1) * 128], ident[:]),
                             reads=[bhb[k2]], writes=[pb[bk]])
                    evac(hT[:, :, j * 128:(j + 1) * 128], psT.rearrange("p (k t) -> p k t", k=8), [pb[bk]], [bhT[j]])
                pend = []
                for j in range(4):
                    blk = 4 * c + j
                    k2 = nblk % 2
                    nblk += 1
                    rmsnorm(nx, xres[:, blk, :], [bxres[blk]], gbc[:], hb[k2][:], [bhb[k2]], 1024)
                    pend.append((j, k2))
                    if len(pend) > 1:
                        ffn_stage2(*pend.pop(0))
                for p_ in pend:
                    ffn_stage2(*p_)
                for (fb0, nf) in fgroups():
                    sl = nstream % 2
                    nstream += 1
                    S.dma("pool", dwg[sl], wg[sl][:, :, 0:nf * 128], gview[:, :, fb0 * 128:(fb0 + nf) * 128], writes=[bwg[sl]])
                    S.dma("pool", dwu[sl], wu[sl][:, :, 0:nf * 128], uview[:, :, fb0 * 128:(fb0 + nf) * 128], writes=[bwu[sl]])
                    if c == 0 and nstream == 2:
                        ev_ = None
                        for k0 in range(0, NFB, 4):
                            k1 = min(NFB, k0 + 4)
                            ev_ = S.dma("pool", dwd, wd[:, k0:k1, :], wdview[:, k0:k1, :])
                        bwd.w = ev_
                    for f in range(nf):
                        fb = fb0 + f
                        gbk = 2 + (nfb % 2)
                        ubk = 4 + (nfb % 2)
                        s2 = nfb % 2
                        nfb += 1
                        for k in range(8):
                            mm(ps[gbk][:], wg[sl][:, k, f * 128:(f + 1) * 128], hT[:, k, :], k == 0, k == 7, [bwg[sl]] + bhT, [pb[gbk]])
                        for k in range(8):
                            mm(ps[ubk][:], wu[sl][:, k, f * 128:(f + 1) * 128], hT[:, k, :], k == 0, k == 7, [bwu[sl]] + bhT, [pb[ubk]])
                        S.op("act", lambda e, o=sg[s2], i=ps[gbk]: e.activation(out=o[:], in_=i[:], func=AF.Silu),
                             reads=[pb[gbk]], writes=[bsg[s2]])
                        S.op("dve", lambda e, o=aT[:, fb, :], a=sg[s2], b=ps[ubk]: e.tensor_tensor(out=o, in0=a[:], in1=b[:], op=ALU.mult),
                             reads=[bsg[s2], pb[ubk]], writes=[baT[fb]])
                for j in range(4):
                    blk = 4 * c + j
                    for half in range(2):
                        bk = 6 + (ndown % 2)
                        ndown += 1
                        for fb in range(NFB):
                            mm(ps[bk][:], aT[:, fb, j * 128:(j + 1) * 128], wd[:, fb, half * 512:(half + 1) * 512],
                               fb == 0, fb == NFB - 1, [baT[fb], bwd], [pb[bk]])
                        S.op("dve", lambda e, o=xres[:, blk, half * 512:(half + 1) * 512], p_=ps[bk]: e.tensor_tensor(out=o, in0=o, in1=p_[:], op=ALU.add),
                             reads=[pb[bk]], writes=[bxres[blk]])
            S.barrier()

        def outproj_residual(st, oT, w_dram, xres, bxres, tag):
            w, bw = load_w(st, tag + "_w", w_dram, 8, 1024)
            n = 0
            for blk in range(NB):
                for half in range(2):
                    bk = n % 4
                    n += 1
                    for k in range(8):
                        mm(ps[bk][:], oT[:, k, blk * 128:(blk + 1) * 128], w[:, k, half * 512:(half + 1) * 512],
                           k == 0, k == 7, [bw], [pb[bk]])
                    S.op("dve", lambda e, o=xres[:, blk, half * 512:(half + 1) * 512], p_=ps[bk]: e.tensor_tensor(out=o, in0=o, in1=p_[:], op=ALU.add),
                         reads=[pb[bk], bxres[blk]], writes=[bxres[blk]])

        def load_xres(st, src):
            xres = sb(st, "xres", [128, NB, 1024], F32)
            bx = [Buf() for _ in range(NB)]
            dl = S.dma_sem()
            ev = None
            for blk in range(NB):
                ev = S.dma("sp", dl, xres[:, blk, :], src[blk])
            for b_ in bx:
                b_.w = ev
            return xres, bx

        def layer0(x_own, sbmask_d, swa_bias, swa_mask, x1_d, kvc=None):
            with _Scope() as l0:
              hole0 = AR.mark()
              oT = sb(l0, "oT", [128, 8, NB * 128], BF16)
              hole1 = AR.mark()
              with _Scope() as att:
                kTa = sb(l0, "kTa", [128, 4, NBA * 128], BF16)
                va = sb(l0, "va", [128, NBA, 512], BF16)
                qTa = sb(l0, "qTa", [128, 4, NB * 128], BF16)
                att_sb_mark = AR.mark()
                kz = sb(l0, "kz", [128, 4, NBA * 128], BF16)
                vb = sb(l0, "vb", [128, NBA, 2, 72], BF16)
                qTb = sb(l0, "qTb", [128, 4, NB * 128], BF16)
                S.op("pool", lambda e: e.memset(vb[:], 1.0))
                S.barrier()
                if kvc is not None and kvc[0] == "load":
                    dkv = S.dma_sem()
                    S.lazy.add(id(dkv))
                    for t_sb, t_dr in zip((kTa, va, kz, vb), kvc[1]):
                        for k0 in range(0, t_sb.shape[1], 8):
                            k1 = min(t_sb.shape[1], k0 + 8)
                            S.dma("sp", dkv, t_sb[:, k0:k1], t_dr[:, k0:k1])
                with _Scope() as st:
                  if not (kvc is not None and kvc[0] == "load"):
                    gbc = load_gbc(st, "gbc_a0", 0)
                    wkv, bwkv = load_w(st, "wkv", w_kv0, 8, 1664)
                    tp = TokPass(st, "p1a", gbc, 0, (hole0, hole1))
                    S.barrier()
                    n = 0
                    for c in range(8):
                        hT, bhT = tp.chunk(x_all, c)
                        for pr in range(4):
                            bk = 2 + n % 4
                            n += 1
                            for k in range(8):
                                mm(ps[bk][:], wkv[:, k, pr * 128:(pr + 1) * 128], hT[:, k, :], k == 0, k == 7, bhT, [pb[bk]])
                            evac(kTa[:, pr, c * 512:(c + 1) * 512], ps[bk][:], [pb[bk]], [])
                        for g in range(4):
                            bk = 2 + n % 4
                            n += 1
                            for k in range(8):
                                mm(ps[bk][:], wkv[:, k, 1024 + g * 128:1024 + (g + 1) * 128], hT[:, k, :], k == 0, k == 7, bhT, [pb[bk]])
                            evac(kz[:, g, c * 512:(c + 1) * 512], ps[bk][:], [pb[bk]], [])
                        for j in range(4):
                            gbk = 4 * c + j
                            bk = 2 + n % 4
                            n += 1
                            for k in range(8):
                                mm(ps[bk][:], hT[:, k, j * 128:(j + 1) * 128], wkv[:, k, 512:1024], k == 0, k == 7, bhT, [pb[bk]])
                            evac(va[:, gbk, :], ps[bk][:], [pb[bk]], [])
                            bk = 2 + n % 4
                            n += 1
                            for k in range(8):
                                mm(ps[bk][:, 0:128], hT[:, k, j * 128:(j + 1) * 128], wkv[:, k, 1536:1664], k == 0, k == 7, bhT, [pb[bk]])
                            evac(vb[:, gbk, :, 0:64], ps[bk][:, 0:128].rearrange("p (g d) -> p g d", g=2), [pb[bk]], [])
                    S.barrier()
                    if kvc is not None and kvc[0] == "save":
                        dkv = S.dma_sem()
                        S.lazy.add(id(dkv))
                        for t_sb, t_dr in zip((kTa, va, kz, vb), kvc[1]):
                            for k0 in range(0, t_sb.shape[1], 8):
                                k1 = min(t_sb.shape[1], k0 + 8)
                                S.dma("sp", dkv, t_dr[:, k0:k1], t_sb[:, k0:k1])
                with _Scope() as st:
                  if STOP >= 2:
                    gbc = load_gbc(st, "gbc_a0b", 0)
                    wq, bwq = load_w(st, "wq", w_q0, 8, 1024)
                    tp = TokPass(st, "p1b", gbc, 0, (hole0, hole1))
                    S.barrier()
                    n = 0
                    for c in range(4):
                        hT, bhT = tp.chunk(x_own, c)
                        for pr in range(8):
                            bk = 2 + n % 4
                            n += 1
                            for k in range(8):
                                mm(ps[bk][:], wq[:, k, pr * 128:(pr + 1) * 128], hT[:, k, :], k == 0, k == 7, bhT, [pb[bk]])
                            dst = qTa if pr < 4 else qTb
                            evac(dst[:, pr % 4, c * 512:(c + 1) * 512], ps[bk][:], [pb[bk]], [], scale=0.125)
                    S.lazy.clear()
                    S.barrier()
                with _Scope() as st:
                  if STOP >= 4:
                    bm = sb(st, "swa_bm", [128, 6 * 512], F32)
                    ltb = sb(st, "swa_ltb", [128, 2, 3 * 512], F32)
                    mk_ = ltb.rearrange("p a n -> p (a n)")
                    S.dma("sp", cd, bm[:], swa_bias[:, :])
                    S.dma("sp", cd, mk_[:], swa_mask[:, :])
                    esk = sb(st, "esk", [128, 8], F32)
                    S.dma("sp", cd, esk[:], sinks_d[0:1, :].partition_broadcast(128))
                    S.barrier()
                    S.op("dve", lambda e: e.tensor_tensor(out=bm[:], in0=bm[:], in1=mk_[:], op=ALU.add))
                    S.op("act", lambda e: e.activation(out=esk[:], in_=esk[:], func=AF.Exp))
                    S.barrier()
                    lt = [ltb[:, i, :] for i in range(2)]
                    blt = [Buf() for _ in range(2)]
                    ws = [sb(st, "swa_w%d" % i, [128, 3 * 512], BF16) for i in range(2)]
                    bws = [Buf() for _ in range(2)]
                    rden = sb(st, "swa_rden", [128, 512], F32)
                    brden = Buf()
                    ouf = sb(st, "swa_ouf", [128, 512], F32)
                    bouf = Buf()
                    otb = [sb(st, "swa_otb%d" % i, [128, 256], BF16) for i in range(2)]
                    botb = [Buf() for _ in range(2)]
                    dsh = [S.dma_sem() for _ in range(2)]
                    def swa_stage_a(i, g, s2):
                        rs = (1, 2) if i == 0 else (0, 1, 2)
                        sbk = (0, 1, 2) if s2 == 0 else (3, 6, 7)
                        for u in range(2):
                            for r in rs:
                                kbg = 2 * i - 1 + r
                                for a in range(2):
                                    mm(ps[sbk[r]][:, u * 256 + a * 128:u * 256 + (a + 1) * 128],
                                       kz[:, 2 * g + u, kbg * 128:(kbg + 1) * 128],
                                       qTb[:, 2 * g + a, i * 128:(i + 1) * 128],
                                       True, True, [], [pb[sbk[r]]])
                        for r in rs:
                            S.op("dve", lambda e, o=lt[s2][:, r * 512:(r + 1) * 512], p_=ps[sbk[r]], b_=bm[:, (g * 3 + r) * 512:(g * 3 + r + 1) * 512]:
                                 e.tensor_tensor(out=o, in0=p_[:], in1=b_, op=ALU.add),
                                 reads=[pb[sbk[r]]], writes=[blt[s2]])
                        lo_, hi_ = rs[0] * 512, 3 * 512
                        S.op("act", lambda e, o=ws[s2][:, lo_:hi_], i_=lt[s2][:, lo_:hi_]: e.activation(out=o, in_=i_, func=AF.Exp),
                             reads=[blt[s2]], writes=[bws[s2]])
                        ob = 4 + s2
                        for q in range(4):
                            for r in rs:
                                kbg = 2 * i - 1 + r
                                mm(ps[ob][0:65, q * 128:(q + 1) * 128], vb[:, kbg, g, 0:65], ws[s2][:, r * 512 + q * 128:r * 512 + (q + 1) * 128],
                                   r == rs[0], r == 2, [bws[s2]], [pb[ob]])

                    def swa_stage_b(i, g, s2):
                        ob = 4 + s2
                        for q in range(4):
                            hd = 4 * g + (0, 2, 1, 3)[q]
                            S.op("dve", lambda e, p_=ps[ob], q=q, hd=hd: e.tensor_scalar(out=rden[64:65, q * 128:(q + 1) * 128], in0=p_[64:65, q * 128:(q + 1) * 128],
                                                                                    scalar1=esk[64:65, hd:hd + 1], scalar2=None, op0=ALU.add),
                                 reads=[pb[ob]], writes=[brden])
                        S.op("act", lambda e: e.activation(out=rden[64:65, :], in_=rden[64:65, :], func=AF.Ln), reads=[brden], writes=[brden])
                        S.op("act", lambda e: e.activation(out=rden[64:65, :], in_=rden[64:65, :], func=AF.Exp, scale=-1.0), reads=[brden], writes=[brden])
                        S.op("act", lambda e, p_=ps[ob]: e.activation(out=ouf[0:64, :], in_=p_[0:64, :], func=AF.Copy), reads=[pb[ob]], writes=[bouf])
                        mm(ps[ob][0:64, :], onesf[64:65, 0:64], rden[64:65, :], True, True, [brden], [pb[ob]])
                        for a in range(2):
                            S.op("dve", lambda e, a=a, p_=ps[ob]: e.tensor_tensor(out=oT[0:64, 4 + 2 * g + a, i * 128:(i + 1) * 128],
                                                                            in0=ouf[0:64, a * 128:(a + 1) * 128],
                                                                            in1=p_[0:64, a * 128:(a + 1) * 128], op=ALU.mult),
                                 reads=[bouf, pb[ob]], writes=[])
                        S.op("dve", lambda e, o=otb[s2], p_=ps[ob]: e.tensor_tensor(out=o[0:64, :], in0=ouf[0:64, 256:512], in1=p_[0:64, 256:512], op=ALU.mult),
                             reads=[bouf, pb[ob]], writes=[botb[s2]])
                        S.dma("sp", dsh[s2], oT[64:128, 4 + 2 * g:4 + 2 * g + 2, i * 128:(i + 1) * 128],
                              otb[s2][0:64, :].rearrange("p (a t) -> p a t", a=2), reads=[botb[s2]])

                    its = [(i, g) for i in range(NB) for g in range(2)]
                    for n_, (i, g) in enumerate(its):
                        swa_stage_a(i, g, n_ % 2)
                        if n_ >= 1:
                            swa_stage_b(its[n_ - 1][0], its[n_ - 1][1], (n_ - 1) % 2)
                    swa_stage_b(its[-1][0], its[-1][1], (len(its) - 1) % 2)
                    S.barrier()
                AR.release(att_sb_mark)
                with _Scope() as st:
                  if STOP >= 3 and not os.environ.get('KSKIP_SB'):
                    negtri = sb(st, "negtri", [128, 128], BF16)
                    negones = sb(st, "negones", [128, 128], BF16)
                    sbmask = sb(st, "sbmask", [128, 256], BF16)
                    S.dma("pool", cdp, negtri[:], tri_d[:, :])
                    S.dma("pool", cdp, sbmask[:], sbmask_d[:, :])
                    S.op("pool", lambda e: e.memset(negones[:], -1.0))
                    sp_all = sb(st, "sp_all", [128, NBA, 512], BF16)
                    bsp = [Buf() for _ in range(NBA)]
                    lsum = sb(st, "lsum", [128, 512], BF16)
                    blsum = Buf()
                    wt = [sb(st, "wt%d" % i, [128, 512], BF16) for i in range(6)]
                    bwt = [Buf() for _ in range(6)]
                    qz = [sb(st, "qz%d" % i, [128, 512], BF16) for i in range(2)]
                    bqz = [Buf() for _ in range(2)]
                    for i_ in range(2):
                        S.op("pool", lambda e, t_=qz[i_]: e.memset(t_[:], 0.0))
                    S.barrier()
                    nA = 0
                    nB = 0
                    nW = 0
                    bankA = (0, 1, 6)
                    bankB = (2, 3, 7)
                    for c in range(4):
                        for h in range(8):
                            pr, u = h // 2, h % 2
                            r0, r1 = 64 * u, 64 * u + 64
                            nkb = 8 * c + 8
                            qs = c * 512
                            S.op("dve", lambda e, o=qz[u][r0:r1, :], i=qTa[r0:r1, pr, qs:qs + 512]: e.tensor_copy(out=o, in_=i),
                                 reads=[], writes=[bqz[u]])

                            def zmm(bank, kb, lo):
                                mm(ps[bank][:, lo:512], kTa[:, pr, kb * 128:(kb + 1) * 128], qz[u][:, lo:512],
                                   True, False, [bqz[u]], [pb[bank]], skip_group_check=True)

                            def maskmm(bank, kb, last):
                                if kb >= 8 * c:
                                    j = (kb - 8 * c) // 2
                                    m = (kb - 8 * c) % 2
                                    mm(ps[bank][:, j * 128:(j + 1) * 128], ident[:], sbmask[:, m * 128:(m + 1) * 128],
                                       False, last, [], [pb[bank]], skip_group_check=True)

                            for kb in range(nkb):
                                lo = 128 * max(0, (kb - 8 * c) // 2)
                                bank = bankA[nA % 3]
                                nA += 1
                                zmm(bank, kb, lo)
                                maskmm(bank, kb, True)
                                S.op("act", lambda e, o=sp_all[:, kb, lo:512], i=ps[bank][:, lo:512]: e.activation(out=o, in_=i, func=AF.Softplus),
                                     reads=[pb[bank]], writes=[bsp[kb]])
                            S.op("pool", lambda e: e.memset(lsum[:], 0.0), writes=[blsum])
                            pvb = 4 + h % 2
                            pend = []
                            for kb in range(nkb - 1, -1, -1):
                                lo = 128 * max(0, (kb - 8 * c) // 2)
                                bank = bankB[nB % 3]
                                nB += 1
                                zmm(bank, kb, lo)
                                mm(ps[bank][:, lo:512], negtri[:], sp_all[:, kb, lo:512], False, False, [bsp[kb]], [pb[bank]], skip_group_check=True)
                                if kb < nkb - 1:
                                    mm(ps[bank][:, lo:512], negones[:], lsum[:, lo:512], False, False, [blsum], [pb[bank]], skip_group_check=True)
                                maskmm(bank, kb, True)
                                w = nW % 6
                                nW += 1
                                S.op("act", lambda e, o=wt[w][:, lo:512], i=ps[bank][:, lo:512]: e.activation(out=o, in_=i, func=AF.Exp),
                                     reads=[pb[bank]], writes=[bwt[w]])
                                pend.append((ps[pvb][:, lo:512], va[:, kb, pr * 128:(pr + 1) * 128], wt[w][:, lo:512],
                                             kb == nkb - 1, kb == 0, [bwt[w]], [pb[pvb]]))
                                if len(pend) > 3:
                                    a_ = pend.pop(0)
                                    mm(*a_, skip_group_check=True)
                                if kb > 0:
                                    S.op("dve", lambda e, o=lsum[:, lo:512], s_=sp_all[:, kb, lo:512]: e.tensor_tensor(out=o, in0=o, in1=s_, op=ALU.add),
                                         reads=[bsp[kb], blsum], writes=[blsum])
                            for a_ in pend:
                                mm(*a_, skip_group_check=True)
                            evac(oT[r0:r1, pr, qs:qs + 512], ps[pvb][r0:r1, :], [pb[pvb]], [], eng="dve")
                    S.barrier()
              with _Scope() as st:
                    xres, bxres = load_xres(st, x_own)
                    with _Scope() as st2:
                      if STOP >= 5:
                        outproj_residual(st2, oT, w_out0, xres, bxres, "op0")
                        S.barrier()
                    with _Scope() as st2:
                      if STOP >= 6:
                        ffn_layer(st2, 0, xres, bxres, (hole0, hole1))
                    dxo = S.dma_sem()
                    for blk in range(NB):
                        S.dma("sp", dxo, x1_d[blk], xres[:, blk, :])
                    S.barrier()

        if do0:
            if mode == "R":
                kvd = (nc.dram_tensor("kv_kTa", [128, 4, NBA * 128], BF16, kind="Internal").ap(),
                       nc.dram_tensor("kv_va", [128, NBA, 512], BF16, kind="Internal").ap(),
                       nc.dram_tensor("kv_kz", [128, 4, NBA * 128], BF16, kind="Internal").ap(),
                       nc.dram_tensor("kv_vb", [128, NBA, 2, 72], BF16, kind="Internal").ap())
                layer0(x_own, sbmask_d, swa_bias, swa_mask, x1_d, ("save", kvd))
                layer0(x_oth, sbmask2_d, swa_bias2, swa_mask2, x1_o, ("load", kvd))
            else:
                layer0(x_own, sbmask_d, swa_bias, swa_mask, x1_d)

        if do1:
            def tt(eng, out, in0, in1, op, reads, writes):
                return S.op(eng, lambda e: e.tensor_tensor(out=out, in0=in0, in1=in1, op=op), reads=reads, writes=writes)

            def act(out, in_, func, reads, writes, **kw):
                return S.op("act", lambda e: e.activation(out=out, in_=in_, func=func, **kw), reads=reads, writes=writes)

            def ts(eng, out, in0, s1, s2, op0, op1, reads, writes):
                if op1 is None:
                    return S.op(eng, lambda e: e.tensor_scalar(out=out, in0=in0, scalar1=s1, scalar2=None, op0=op0), reads=reads, writes=writes)
                return S.op(eng, lambda e: e.tensor_scalar(out=out, in0=in0, scalar1=s1, scalar2=s2, op0=op0, op1=op1), reads=reads, writes=writes)

            def cp(eng, out, in_, reads, writes):
                return S.op(eng, lambda e: e.tensor_copy(out=out, in_=in_), reads=reads, writes=writes)

            def tr(out, in_, reads, writes):
                return S.op("pe", lambda e: e.transpose(out, in_, ident[:]), reads=reads, writes=writes)

            SCALE = 96.0 ** -0.5
            with _Scope() as l1:
              hole0 = AR.mark()
              oT1 = sb(l1, "oT1", [128, 8, NB * 128], BF16)
              hole1 = AR.mark()
              with _Scope() as att:
                ckvT = sb(att, "ckvT", [128, 2, NBA * 128], BF16)
                krR = sb(att, "krR", [128, NBA * 128], BF16)
                cqT = sb(att, "cqT", [128, 3, NB * 128], BF16)
                cosq = sb(att, "cosq", [128, NB * 128], F32)
                sinq = sb(att, "sinq", [128, NB * 128], F32)
                rcst = sb(att, "rcst", [128, 8], F32)
                mmask = sb(att, "mmask", [128, 256], BF16)
                S.dma("sp", cd, rcst[:], rope_cst_d[:, :])
                S.dma("pool", cdp, mmask[:], mlamask_d[:, :])
                S.barrier()

                def rope_tables(st, pos_row, ntok, cosT, sinT):
                    pi_ = sb(st, "rt_pi", [128, 1024], I32)
                    pf = sb(st, "rt_pf", [128, 1024], F32)
                    u_ = sb(st, "rt_u", [128, 1024], F32)
                    ki = sb(st, "rt_ki", [128, 1024], I32)
                    kf = sb(st, "rt_kf", [128, 1024], F32)
                    bpi, bpf, bu, bki, bkf = Buf(), Buf(), Buf(), Buf(), Buf()
                    dpi = S.dma_sem()
                    R = slice(64, 96)
                    for c0 in range(0, ntok, 1024):
                        S.dma("sp", dpi, pi_[R, :], pos_row[0:1, c0:c0 + 1024].partition_broadcast(32), writes=[bpi])
                        cp("dve", pf[R, :], pi_[R, :], [bpi], [bpf])
                        for (offcol, dst, is_sin) in ((1, sinT, True), (2, cosT, False)):
                            ts("dve", u_[R, :], pf[R, :], rcst[R, 0:1], rcst[R, offcol:offcol + 1], ALU.mult, ALU.add, [bpf], [bu])
                            cp("dve", ki[R, :], u_[R, :], [bu], [bki])
                            cp("dve", kf[R, :], ki[R, :], [bki], [bkf])
                            tt("dve", u_[R, :], u_[R, :], kf[R, :], ALU.subtract, [bkf, bu], [bu])
                            ts("dve", kf[R, :], u_[R, :], 0.5, None, ALU.is_gt, None, [bu], [bkf])
                            tt("dve", u_[R, :], u_[R, :], kf[R, :], ALU.subtract, [bkf, bu], [bu])
                            if is_sin:
                                act(dst[R, c0:c0 + 1024], u_[R, :], AF.Sin, [bu], [], scale=rcst[R, 3:4])
                            else:
                                act(dst[R, c0:c0 + 1024], u_[R, :], AF.Sin, [bu], [], scale=6.283184)

                def down_pass(src_blocks, nblocks, do_q, lat_dst):
                    with _Scope() as st:
                        gbc = load_gbc(st, "gbc_a1", 2)
                        wdn, _b = load_w(st, "wdn", mla_dn, 8, 832)
                        gqb = sb(st, "gqb", [128, 384], F32)
                        S.dma("sp", cd, gqb[:], gq_d[0:1, :].partition_broadcast(128))
                        tp = TokPass(st, "l1p1", gbc, 0, (hole0, hole1))
                        latst = [sb(st, "latst%d" % i, [128, LATW], F32) for i in range(2)]
                        blat = [Buf() for _ in range(2)]
                        dlat = [S.dma_sem() for _ in range(2)]
                        cqn = [sb(st, "cqn%d" % i, [128, 384], BF16) for i in range(2)]
                        bcqn = [Buf() for _ in range(2)]
                        nxq = NormCtx(st, "nxq", 384, 3)
                        S.barrier()
                        n = 0
                        dpend = []
                        for c in range(nblocks // 4):
                            hT, bhT = tp.chunk(src_blocks, c)
                            for j in range(4):
                                blk = 4 * c + j
                                s2 = n % 2
                                A, B_ = 2 + 2 * s2, 3 + 2 * s2
                                n += 1
                                for k in range(8):
                                    mm(ps[A][:], hT[:, k, j * 128:(j + 1) * 128], wdn[:, k, 0:512], k == 0, k == 7, bhT, [pb[A]])
                                for k in range(8):
                                    mm(ps[B_][:, 0:320], hT[:, k, j * 128:(j + 1) * 128], wdn[:, k, 512:832], k == 0, k == 7, bhT, [pb[B_]])
                                if lat_dst is not None:
                                    evac(latst[s2][:, 0:128], ps[A][:, 384:512], [pb[A]], [blat[s2]])
                                    evac(latst[s2][:, 128:LATW], ps[B_][:, 0:320], [pb[B_]], [blat[s2]])
                                    S.dma("sp", dlat[s2], lat_dst[blk], latst[s2][:], reads=[blat[s2]])
                                if do_q:
                                    rmsnorm(nxq, ps[A][:, 0:384], [pb[A]], gqb[:], cqn[s2][:], [bcqn[s2]], 384)

                                    def dp_stage_y(s2=s2, blk=blk):
                                        tb = 6 + s2
                                        psT = ps[tb][:].bitcast(BF16)
                                        for k in range(3):
                                            tr(psT[:, k * 128:(k + 1) * 128], cqn[s2][:, k * 128:(k + 1) * 128], [bcqn[s2]], [pb[tb]])
                                        evac(cqT[:, :, blk * 128:(blk + 1) * 128], psT[:, 0:384].rearrange("p (k t) -> p k t", k=3), [pb[tb]], [])
                                    dpend.append(dp_stage_y)
                                    if len(dpend) > 1:
                                        dpend.pop(0)()
                        for f_ in dpend:
                            f_()
                        S.barrier()

                if mode == "B":
                    down_pass(x1_eo, NBA, False, lat_all)
                    down_pass(x1_d, NB, True, None)
                elif mode == "R":
                    down_pass(x1_d, NB, True, lat_all[0:NB])
                    down_pass(x1_o, NB, False, lat_all[NB:NBA])
                else:
                    down_pass(x1_d, NB, True, lat_own.rearrange("(b p) n -> b p n", p=128))
                    dcc = S.dma_sem()
                    dcc[1] += 16 if not os.environ.get("KNOCC") else 0
                    if not os.environ.get("KNOCC"):
                      S.ops["pool"].append(([], (lambda e: e.collective_compute("AllGather", op=ALU.bypass, replica_groups=CCG,
                                                                             ins=[lat_own[:, :]], outs=[lat_cc[:, :]])), (dcc[0], 16)))
                    S.barrier()

                with _Scope() as st:
                    cosk = sb(st, "cosk", [128, NBA * 128], F32)
                    sink = sb(st, "sink", [128, NBA * 128], F32)
                    with _Scope() as st2:
                      if STOP1 >= 2:
                        rope_tables(st2, pos_all, NBA * 128, cosk, sink)
                        if mode == "R":
                            S.barrier()
                            cp("pool", cosq[64:96, :], cosk[64:96, 0:NB * 128], [], [])
                            cp("pool", sinq[64:96, :], sink[64:96, 0:NB * 128], [], [])
                        else:
                            rope_tables(st2, pos_own, NB * 128, cosq, sinq)
                        S.barrier()
                    gkvb = sb(st, "gkvb", [128, 256], F32)
                    S.dma("sp", cd, gkvb[:], gkv_d[0:1, :].partition_broadcast(128))
                    latc = [sb(st, "latc%d" % i, [128, LATW], F32) for i in range(3)]
                    blc = [Buf() for _ in range(3)]
                    dlc = [S.dma_sem() for _ in range(3)]
                    ckvn = [sb(st, "ckvn%d" % i, [128, 256], BF16) for i in range(2)]
                    bckvn = [Buf() for _ in range(2)]
                    krb = [sb(st, "krb%d" % i, [128, 256], BF16) for i in range(2)]
                    bkrb = [Buf() for _ in range(2)]
                    t1 = [sb(st, "kt1_%d" % i, [128, 128], F32) for i in range(2)]
                    t2 = [sb(st, "kt2_%d" % i, [128, 128], F32) for i in range(2)]
                    bt1 = [Buf() for _ in range(2)]
                    bt2 = [Buf() for _ in range(2)]
                    nxk = NormCtx(st, "nxk", 256, 3)
                    S.barrier()
                    R = slice(64, 96)
                    def p2_stage_y(blk):
                        s2 = blk % 2
                        cols = slice(blk * 128, (blk + 1) * 128)
                        tb = s2
                        psT = ps[tb][:].bitcast(BF16)
                        tr(psT[:, 0:128], ckvn[s2][:, 0:128], [bckvn[s2]], [pb[tb]])
                        tr(psT[:, 128:256], ckvn[s2][:, 128:256], [bckvn[s2]], [pb[tb]])
                        tr(psT[:, 256:384], krb[s2][:, 0:128], [bkrb[s2]], [pb[tb]])
                        tr(psT[:, 384:512], krb[s2][:, 128:256], [bkrb[s2]], [pb[tb]])
                        evac(ckvT[:, :, cols], psT[:, 0:256].rearrange("p (k t) -> p k t", k=2), [pb[tb]], [])
                        tt("dve", t1[s2][R, :], psT[R, 256:384], cosk[R, cols], ALU.mult, [pb[tb]], [bt1[s2]])
                        tt("dve", t2[s2][R, :], psT[R, 384:512], sink[R, cols], ALU.mult, [pb[tb]], [bt2[s2]])
                        tt("dve", krR[R, cols], t1[s2][R, :], t2[s2][R, :], ALU.add, [bt1[s2], bt2[s2]], [])

                    for blk in range(NBA if STOP1 >= 3 else 0):
                        s3, s2 = blk % 3, blk % 2
                        S.dma("sp", dlc[s3], latc[s3][:], lat_all[blk], writes=[blc[s3]])
                        rmsnorm(nxk, latc[s3][:, 0:256], [blc[s3]], gkvb[:], ckvn[s2][:], [bckvn[s2]], 256)
                        cp("dve", krb[s2][:], latc[s3][:, 192:LATW], [blc[s3]], [bkrb[s2]])
                        if blk >= 1:
                            p2_stage_y(blk - 1)
                    if STOP1 >= 3:
                        p2_stage_y(NBA - 1)
                    S.barrier()

                with _Scope() as st:
                    wuq, _b = load_w(st, "wuq", mla_uq, 3, 1568)
                    wuqs, _b = load_w(st, "wuqs", mla_uqs, 3, 1568)
                    wkn, _b = load_w(st, "wkn", mla_kn, 2, 1024)
                    wv, _b = load_w(st, "wv", mla_v, 2, 1024)
                    vgf = sb(st, "vg", [128, NBA * 4 * 72 + 64], BF16)
                    S.op("pool", lambda e: e.memset(vgf[:], 1.0))
                    vg = vgf[:, 0:NBA * 4 * 72].rearrange("p (b h d) -> p b h d", b=NBA, h=4)
                    kTh = [sb(st, "kTh%d" % i, [128, NBA * 128], BF16) for i in range(2)]
                    qTh = [sb(st, "qTh%d" % i, [128, NB * 128], BF16) for i in range(2)]
                    for i_ in range(2):
                        S.op("pool", lambda e, t_=kTh[i_]: e.memset(t_[:], 0.0))
                        S.op("pool", lambda e, t_=qTh[i_]: e.memset(t_[:], 0.0))
                    bkTh = [Buf() for _ in range(2)]
                    bqTh = [Buf() for _ in range(2)]
                    bvg = Buf()
                    wt = [sb(st, "mwt%d" % i, [128, 512], BF16) for i in range(6)]
                    bwt = [Buf() for _ in range(6)]
                    q1 = [sb(st, "q1_%d" % i, [128, 512], F32) for i in range(2)]
                    q2 = [sb(st, "q2_%d" % i, [128, 512], F32) for i in range(2)]
                    bq1 = [Buf() for _ in range(2)]
                    bq2 = [Buf() for _ in range(2)]
                    rden = sb(st, "m_rden", [128, 512], F32)
                    brden = Buf()
                    ouf = sb(st, "m_ouf", [128, 512], F32)
                    bouf = Buf()
                    otb = [sb(st, "m_otb%d" % i, [128, 512], BF16) for i in range(2)]
                    botb = [Buf() for _ in range(2)]
                    dsh = [S.dma_sem() for _ in range(2)]
                    S.barrier()
                    R = slice(64, 96)
                    npre_, nS_, nW_, nq_, nodd_ = [0], [0], [0], [0], [0]
                    def head_v(h):
                        grp, hh = h // 4, h % 4
                        if True:
                            for blk in range(NBA):
                                bk = 6 + npre_[0] % 2
                                npre_[0] += 1
                                for k in range(2):
                                    mm(ps[bk][:, 0:256], ckvT[:, k, blk * 128:(blk + 1) * 128], wv[:, k, grp * 256:(grp + 1) * 256],
                                       k == 0, k == 1, [], [pb[bk]])
                                evac(vg[:, blk, :, 0:64], ps[bk][:, 0:256].rearrange("p (a d) -> p a d", a=4), [pb[bk]], [bvg])
                    def head_kq(h):
                        grp, hh = h // 4, h % 4
                        hs = h % 2
                        for c8 in range(8):
                            bk = 6 + npre_[0] % 2
                            npre_[0] += 1
                            for k in range(2):
                                mm(ps[bk][0:64, :], wkn[:, k, h * 64:(h + 1) * 64], ckvT[:, k, c8 * 512:(c8 + 1) * 512], k == 0, k == 1, [], [pb[bk]])
                            evac(kTh[hs][0:64, c8 * 512:(c8 + 1) * 512], ps[bk][0:64, :], [pb[bk]], [bkTh[hs]])
                        cp("dve", kTh[hs][R, :], krR[R, :], [], [bkTh[hs]])
                        for c in range(4):
                            cols = slice(c * 512, (c + 1) * 512)
                            bM, bS = 6, 7
                            for k in range(3):
                                mm(ps[bM][:], wuq[:, k, h * 96:h * 96 + 128], cqT[:, k, cols], k == 0, k == 2, [], [pb[bM]])
                            for k in range(3):
                                mm(ps[bS][:], wuqs[:, k, h * 96:h * 96 + 128], cqT[:, k, cols], k == 0, k == 2, [], [pb[bS]])
                            evac(qTh[hs][0:64, cols], ps[bM][0:64, :], [pb[bM]], [bqTh[hs]])
                            s2 = nq_[0] % 2
                            nq_[0] += 1
                            tt("dve", q1[s2][R, :], ps[bM][R, :], cosq[R, cols], ALU.mult, [pb[bM]], [bq1[s2]])
                            tt("dve", q2[s2][R, :], ps[bS][R, :], sinq[R, cols], ALU.mult, [pb[bS]], [bq2[s2]])
                            tt("dve", qTh[hs][R, cols], q1[s2][R, :], q2[s2][R, :], ALU.add, [bq1[s2], bq2[s2]], [bqTh[hs]])
                    def head_att(h):
                        grp, hh = h // 4, h % 4
                        hs = h % 2
                        for c in range(4):
                            qs = c * 512
                            blocks = [(0, e_) for e_ in range(4 * c + 4)] + [(1, o_) for o_ in range(4 * c + 4)]
                            pvb = 3 + (4 * h + c) % 2
                            pend = []
                            for bi, (par, idx) in enumerate(blocks):
                                kb = idx if par == 0 else NB + idx
                                j0 = max(0, idx - 4 * c)
                                lo = 128 * j0
                                bank = nS_[0] % 3
                                nS_[0] += 1
                                diag = idx >= 4 * c
                                mm(ps[bank][:, lo:512], kTh[hs][:, kb * 128:(kb + 1) * 128], qTh[hs][:, qs + lo:qs + 512],
                                   True, not diag, [bkTh[hs], bqTh[hs]], [pb[bank]], skip_group_check=True)
                                if diag:
                                    mm(ps[bank][:, lo:lo + 128], ident[:], mmask[:, par * 128:(par + 1) * 128], False, True, [], [pb[bank]], skip_group_check=True)
                                w = nW_[0] % 6
                                nW_[0] += 1
                                act(wt[w][:, lo:512], ps[bank][:, lo:512], AF.Exp, [pb[bank]], [bwt[w]], scale=SCALE)
                                vo = (kb * 4 + hh) * 72
                                pend.append((ps[pvb][:, lo:512], vgf[:, vo:vo + 128], wt[w][:, lo:512], bi == 0, bi == len(blocks) - 1,
                                             [bwt[w], bvg], [pb[pvb]]))
                                if len(pend) > 3:
                                    a_ = pend.pop(0)
                                    mm(*a_, skip_group_check=True)
                            for a_ in pend:
                                mm(*a_, skip_group_check=True)
                            act(rden[64:65, :], ps[pvb][64:65, :], AF.Ln, [pb[pvb]], [brden])
                            act(rden[64:65, :], rden[64:65, :], AF.Exp, [brden], [brden], scale=-1.0)
                            mm(ps[5][0:64, :], onesf[64:65, 0:64], rden[64:65, :], True, True, [brden], [pb[5]])
                            act(ouf[0:64, :], ps[pvb][0:64, :], AF.Copy, [pb[pvb]], [bouf])
                            if h % 2 == 0:
                                tt("dve", oT1[0:64, h // 2, qs:qs + 512], ouf[0:64, :], ps[5][0:64, :], ALU.mult, [bouf, pb[5]], [])
                            else:
                                so = nodd_[0] % 2
                                nodd_[0] += 1
                                tt("dve", otb[so][0:64, :], ouf[0:64, :], ps[5][0:64, :], ALU.mult, [bouf, pb[5]], [botb[so]])
                                S.dma("sp", dsh[so], oT1[64:128, h // 2, qs:qs + 512], otb[so][0:64, :], reads=[botb[so]])

                    for h in range(16 if STOP1 >= 4 else 0):
                        if h % 4 == 0:
                            head_v(h)
                        if h == 0:
                            head_kq(0)
                        if h + 1 < 16:
                            head_kq(h + 1)
                        head_att(h)
                    S.barrier()
              with _Scope() as st:
                    xres, bxres = load_xres(st, x1_d)
                    with _Scope() as st2:
                        outproj_residual(st2, oT1, mla_wo, xres, bxres, "op1")
                        S.barrier()
                    with _Scope() as st2:
                        ffn_layer(st2, 1, xres, bxres, (hole0, hole1))
                    with _Scope() as st2:
                        gfin = load_gbc(st2, "gfin", 4)
                        S.barrier()
                        ost = [sb(st2, "ost%d" % i, [128, 1024], F32) for i in range(2)]
                        bost = [Buf() for _ in range(2)]
                        dost = [S.dma_sem() for _ in range(2)]
                        nxf = NormCtx(st2, "nxf", 1024, 3)
                        for blk in range(NB):
                            s2 = blk % 2
                            rmsnorm(nxf, xres[:, blk, :], [bxres[blk]], gfin[:], ost[s2][:], [bost[s2]], 1024)
                            S.dma("sp", dost[s2], out_own[blk], ost[s2][:], reads=[bost[s2]])
                        S.barrier()

        S.barrier()
        S.flush(block)
    return nc


def _t5_bucket(rel):
    rel = np.maximum(rel, 0)
    relf = np.maximum(rel, 1).astype(np.float32)
    large = 16 + (np.log(relf / np.float32(16)) / np.float32(math.log(128 / 16)) * np.float32(16)).astype(np.int32)
    large = np.minimum(large, 31)
    return np.where(rel < 16, rel, large)


def _consts(p, rel_bias_table):
    s = np.arange(128)[:, None]
    t = np.arange(128)[None, :]
    c = {}
    c["ident"] = np.eye(128, dtype=np.float32)
    c["negtri"] = np.where(s >= t, -1.0, 0.0).astype(np.float32)
    full = np.zeros((128, 128), np.float32)
    none = np.full((128, 128), NEG, np.float32)
    strict = np.where(s < t, 0.0, NEG).astype(np.float32)
    incl = np.where(s <= t, 0.0, NEG).astype(np.float32)
    c["sbmask"] = np.concatenate([strict, none] if p == 0 else [full, strict], axis=1)
    c["mlamask"] = np.concatenate([incl, none] if p == 0 else [full, incl], axis=1)
    bias = np.zeros((128, 2, 3, 4, 128), np.float32)
    mask = np.zeros((128, 2, 3, 4, 128), np.float32)
    for r in range(3):
        rel = (p + 1 - r) * 128 + t - s
        valid = (rel >= 0) & (rel < 128)
        bk = _t5_bucket(rel)
        for g in range(2):
            for q in range(4):
                hd = 4 * g + (0, 2, 1, 3)[q]
                gathered = rel_bias_table[bk, hd]
                bias[:, g, r, q, :] = np.where(valid, gathered, np.float32(0.0))
                mask[:, g, r, q, :] = np.where(valid, 0.0, NEG)
    c["swa_bias"] = bias.reshape(128, 6 * 512)
    c["swa_mask"] = mask.reshape(128, 6 * 512)
    return c


def _prep_common(inp):
    w_in = inp["even_w_in"][0]
    kb0, kb1 = w_in[:, 2048:2112], w_in[:, 2112:2176]
    d = {}
    z64 = np.zeros_like(kb0)
    d["w_kv0"] = np.ascontiguousarray(np.concatenate([w_in[:, 512:1024], w_in[:, 1024:1536], kb0, z64, z64, kb0, kb1, z64, z64, kb1,
                                                      w_in[:, 2176:2304]], axis=1))
    d["w_q0"] = np.ascontiguousarray(np.concatenate([w_in[:, 0:512], w_in[:, 1536:2048]], axis=1))
    d["w_out0"] = np.ascontiguousarray(inp["even_w_out"][0])
    d["gvec"] = np.ascontiguousarray(np.stack([inp["attn_norm"][0], inp["ffn_norm"][0], inp["attn_norm"][1], inp["ffn_norm"][1], inp["final_norm"]]))
    d["ffn_g"] = np.ascontiguousarray(inp["ffn_w_gate"])
    d["ffn_u"] = np.ascontiguousarray(inp["ffn_w_up"])
    d["ffn_d"] = np.ascontiguousarray(inp["ffn_w_down"])
    d["sinks"] = np.ascontiguousarray(inp["even_sinks"][0:1])
    return d


def _prep_l1(inp):
    d = {}
    wd = inp["mla_w_down"][0]
    z32 = np.zeros((1024, 32), np.float32)
    z64 = np.zeros((1024, 64), np.float32)
    d["mla_dn"] = np.ascontiguousarray(np.concatenate([wd[:, 0:384], wd[:, 384:640], wd[:, 640:672], z32, z64,
                                                       wd[:, 656:672], wd[:, 640:656], z32], axis=1))
    uq = inp["mla_w_uq"][0].reshape(384, 16, 96)
    uqs = np.concatenate([uq[:, :, 0:64], uq[:, :, 80:96], uq[:, :, 64:80]], axis=2)
    zp = np.zeros((384, 32), np.float32)
    d["mla_uq"] = np.ascontiguousarray(np.concatenate([uq.reshape(384, 1536), zp], axis=1))
    d["mla_uqs"] = np.ascontiguousarray(np.concatenate([uqs.reshape(384, 1536), zp], axis=1))
    ukv = inp["mla_w_ukv"][0].reshape(256, 16, 128)
    d["mla_kn"] = np.ascontiguousarray(ukv[:, :, 0:64].reshape(256, 1024))
    d["mla_v"] = np.ascontiguousarray(ukv[:, :, 64:128].reshape(256, 1024))
    d["mla_wo"] = np.ascontiguousarray(inp["mla_w_o"][0])
    d["gq"] = np.ascontiguousarray(inp["mla_q_norm"][0:1])
    d["gkv"] = np.ascontiguousarray(inp["mla_kv_norm"][0:1])
    cst = np.zeros((128, 8), np.float32)
    freqs = np.float32(10000.0) ** (-(np.arange(16, dtype=np.float32) / np.float32(16)))
    for r in range(32):
        sgn = -1.0 if r < 16 else 1.0
        cst[64 + r] = [freqs[r % 16] / np.float32(2.0 * PI), 0.0, 0.25, 6.283184 * sgn, 0.0, 0.0, 0.0, 0.0]
    d["rope_cst"] = cst
    return d


B_KEYS = ("gvec", "ident", "ffn_g", "ffn_u", "ffn_d", "mla_dn", "mla_uq", "mla_uqs", "mla_kn", "mla_v", "mla_wo", "gq", "gkv",
          "mlamask", "rope_cst", "pos_own", "pos_all", "x1_own", "x1_eo")


def run_layer1(inputs, x1, cores=tuple(range(8))):
    inp = {k: np.asarray(v) for k, v in inputs.items()}
    common = _prep_common(inp)
    common.update(_prep_l1(inp))
    pos = np.asarray(inputs["positions"]).astype(np.int32)
    in_maps = []
    for core in cores:
        b, p = core // 2, core % 2
        m = dict(common)
        m.update(_consts(p, np.asarray(inputs["rel_bias_table"], np.float32)))
        pb_ = pos[b].reshape(NB, 2, 128)
        m["pos_own"] = np.ascontiguousarray(pb_[:, p].reshape(1, NB * 128))
        m["pos_all"] = np.ascontiguousarray(pb_.transpose(1, 0, 2).reshape(1, NBA * 128))
        m["x1_own"] = np.ascontiguousarray(x1[core])
        m["x1_eo"] = np.ascontiguousarray(np.concatenate([x1[2 * b], x1[2 * b + 1]], axis=0))
        in_maps.append({k: m[k] for k in B_KEYS})
    res = run_bass_kernel_spmd(_get_nc("B"), in_maps, core_ids=list(range(len(cores))))
    return {core: r["out_own"] for core, r in zip(cores, res.results)}


A_KEYS = ("gvec", "ident", "ffn_g", "ffn_u", "ffn_d", "x_own", "x_all", "w_kv0", "w_q0", "w_out0", "negtri", "sbmask",
          "swa_bias", "swa_mask", "sinks")

_CACHE = {}


def _get_nc(mode):
    if mode not in _CACHE:
        _CACHE[mode] = build(mode)
    return _CACHE[mode]


def run_layer0(inputs):
    x = np.asarray(inputs["x"], np.float32)
    common = _prep_common({k: np.asarray(v) for k, v in inputs.items()})
    in_maps = []
    for core in range(8):
        b, p = core // 2, core % 2
        xb = x[b].reshape(NB, 2, 128, 1024)
        m = dict(common)
        m.update(_consts(p, np.asarray(inputs["rel_bias_table"], np.float32)))
        m["x_own"] = np.ascontiguousarray(xb[:, p])
        m["x_all"] = np.ascontiguousarray(x[b].reshape(NBA, 128, 1024))
        in_maps.append({k: m[k] for k in A_KEYS})
    res = run_bass_kernel_spmd(_get_nc("A"), in_maps, core_ids=list(range(8)))
    return [r["x1_own"] for r in res.results]


F_KEYS = tuple(dict.fromkeys(A_KEYS + tuple(k for k in B_KEYS if k not in ("x1_own", "x1_eo"))))


def run_fused(inputs):
    inp = {k: np.asarray(v) for k, v in inputs.items()}
    x = np.asarray(inputs["x"], np.float32)
    common = _prep_common(inp)
    common.update(_prep_l1(inp))
    pos = np.asarray(inputs["positions"]).astype(np.int32)
    in_maps = []
    for core in range(8):
        b, p = core // 2, core % 2
        m = dict(common)
        m.update(_consts(p, np.asarray(inputs["rel_bias_table"], np.float32)))
        xb = x[b].reshape(NB, 2, 128, 1024)
        m["x_own"] = np.ascontiguousarray(xb[:, p])
        m["x_all"] = np.ascontiguousarray(x[b].reshape(NBA, 128, 1024))
        pb_ = pos[b].reshape(NB, 2, 128)
        m["pos_own"] = np.ascontiguousarray(pb_[:, p].reshape(1, NB * 128))
        m["pos_all"] = np.ascontiguousarray(pb_.transpose(1, 0, 2).reshape(1, NBA * 128))
        in_maps.append({k: m[k] for k in F_KEYS})
    res = run_bass_kernel_spmd(_get_nc("F"), in_maps, core_ids=list(range(8)))
    return {core: r["out_own"] for core, r in enumerate(res.results)}


R_KEYS = F_KEYS + ("x_oth", "sbmask2", "swa_bias2", "swa_mask2")


def run_redundant(inputs):
    inp = {k: np.asarray(v) for k, v in inputs.items()}
    x = np.asarray(inputs["x"], np.float32)
    common = _prep_common(inp)
    common.update(_prep_l1(inp))
    pos = np.asarray(inputs["positions"]).astype(np.int32)
    rbt = np.asarray(inputs["rel_bias_table"], np.float32)
    s_ = np.arange(128)[:, None]
    t_ = np.arange(128)[None, :]
    incl = np.where(s_ <= t_, 0.0, NEG).astype(np.float32)
    in_maps = []
    for core in range(8):
        b, p = core // 2, core % 2
        m = dict(common)
        m.update(_consts(p, rbt))
        c2 = _consts(1 - p, rbt)
        m["sbmask2"], m["swa_bias2"], m["swa_mask2"] = c2["sbmask"], c2["swa_bias"], c2["swa_mask"]
        other = np.full((128, 128), NEG, np.float32) if p == 0 else np.zeros((128, 128), np.float32)
        m["mlamask"] = np.concatenate([incl, other], axis=1)
        xb = x[b].reshape(NB, 2, 128, 1024)
        m["x_own"] = np.ascontiguousarray(xb[:, p])
        m["x_oth"] = np.ascontiguousarray(xb[:, 1 - p])
        m["x_all"] = np.ascontiguousarray(x[b].reshape(NBA, 128, 1024))
        pb_ = pos[b].reshape(NB, 2, 128)
        m["pos_own"] = np.ascontiguousarray(pb_[:, p].reshape(1, NB * 128))
        m["pos_all"] = np.ascontiguousarray(np.concatenate([pb_[:, p].reshape(-1), pb_[:, 1 - p].reshape(-1)]).reshape(1, NBA * 128))
        in_maps.append({k: m[k] for k in R_KEYS})
    res = run_bass_kernel_spmd(_get_nc("R"), in_maps, core_ids=list(range(8)))
    return {core: r["out_own"] for core, r in enumerate(res.results)}


FUSED = True


def kernel(**inputs):
    if FUSED:
        outs = run_redundant(inputs)
    else:
        x1l = run_layer0(inputs)
        outs = run_layer1(inputs, {c: x1l[c] for c in range(8)})
    out = np.zeros((4, 4096, 1024), np.float32)
    for core in range(8):
        b, p = core // 2, core % 2
        out[b].reshape(NB, 2, 128, 1024)[:, p] = outs[core]
    return out
```
